# Optimizing a Trainium2 kernel written in Bass

```python
import jax, jax.numpy as jnp
from jax import lax
import numpy as np

D_MODEL = 1024
BATCH = 8
SEQ = 2048
DEPTH = 1
DEC_BATCH = 128
DEC_SEQ = 1
PAST_LEN = 16384
PAGE_SIZE = 128

N_RET_HEADS = 4
RET_DK = 256
RET_DV = 512
RET_QK = N_RET_HEADS * RET_DK
RET_V = N_RET_HEADS * RET_DV
CHUNK = 128
ROPE_BASE = 10000.0
CONV_CH = D_MODEL
CONV_WIDTH = 31
CONV_STATE = CONV_WIDTH - 1
D_FF = -(-(8 * D_MODEL) // (3 * 256)) * 256
EPS = 1e-6
SPLITS = [RET_QK, RET_QK, RET_V, RET_V, CONV_CH, CONV_CH, D_MODEL, D_MODEL]
D_IN = sum(SPLITS)
SPLIT_IDX = [int(v) for v in np.cumsum(SPLITS)[:-1]]

kernel_name = "retention_conformer_parallel_decoder_step"


def rms_norm(x, g):
    xf = x.astype(jnp.float32)
    out = xf * lax.rsqrt(jnp.mean(xf * xf, axis=-1, keepdims=True) + EPS)
    return (out * g.astype(jnp.float32)).astype(x.dtype)


def layer_norm(x, g, b):
    xf = x.astype(jnp.float32)
    mu = jnp.mean(xf, axis=-1, keepdims=True)
    var = jnp.mean(jnp.square(xf - mu), axis=-1, keepdims=True)
    out = (xf - mu) * lax.rsqrt(var + EPS)
    return (out * g.astype(jnp.float32) + b.astype(jnp.float32)).astype(x.dtype)


def rotary(t, pos):
    inv_freq = ROPE_BASE ** (-jnp.arange(0, RET_DK, 2, dtype=jnp.float32) / RET_DK)
    ang = pos[:, None] * inv_freq[None, :]
    cos = jnp.cos(ang)[None, :, None, :]
    sin = jnp.sin(ang)[None, :, None, :]
    t1, t2 = jnp.split(t, 2, axis=-1)
    return jnp.concatenate([t1 * cos - t2 * sin, t2 * cos + t1 * sin], axis=-1)


def retention_log_gamma():
    return jnp.log(1.0 - jnp.exp(jnp.linspace(jnp.log(1.0 / 32.0), jnp.log(1.0 / 512.0), N_RET_HEADS)))


def retention_chunk(q, k, v, s0, log_gamma):
    L = q.shape[2]
    idx = jnp.arange(L, dtype=jnp.float32)
    diff = idx[:, None] - idx[None, :]
    decay = jnp.where(diff >= 0, jnp.exp(log_gamma[:, None, None] * jnp.maximum(diff, 0.0)), 0.0)
    scores = jnp.einsum('bhid,bhjd->bhij', q, k) * decay[None]
    y = jnp.einsum('bhij,bhjv->bhiv', scores, v)
    cross_decay = jnp.exp(log_gamma[:, None] * (idx + 1.0))[None, :, :, None]
    y = y + jnp.einsum('bhid,bhdv->bhiv', q, s0) * cross_decay
    k_dec = k * jnp.exp(log_gamma[:, None] * (L - 1.0 - idx))[None, :, :, None]
    s_new = jnp.exp(log_gamma * L)[None, :, None, None] * s0 + jnp.einsum('bhjd,bhjv->bhdv', k_dec, v)
    return y, s_new


def retention_seq(q, k, v, s0, log_gamma):
    B, H, L, _ = q.shape
    if L <= CHUNK:
        return retention_chunk(q, k, v, s0, log_gamma)
    nc = L // CHUNK

    def split(t):
        return t.reshape(B, H, nc, CHUNK, t.shape[-1]).transpose(2, 0, 1, 3, 4)

    def step(s, xs):
        qc, kc, vc = xs
        y, s = retention_chunk(qc, kc, vc, s, log_gamma)
        return s, y

    s_fin, ys = lax.scan(step, s0, (split(q), split(k), split(v)))
    y = ys.transpose(1, 2, 0, 3, 4).reshape(B, H, L, ys.shape[-1])
    return y, s_fin


def mixer_sublayer(h, pos, conv_prev, ret_prev, w_in, conv_w, conv_b, conv_ln_g, conv_ln_b,
                   w_conv_out, ret_gn_g, w_ret_out, w_out):
    B, L, _ = h.shape
    z = h @ w_in
    q, k, v, g, u, a, gr, gc = jnp.split(z, SPLIT_IDX, axis=-1)
    qf = rotary(q.astype(jnp.float32).reshape(B, L, N_RET_HEADS, RET_DK), pos) * (RET_DK ** -0.5)
    kf = rotary(k.astype(jnp.float32).reshape(B, L, N_RET_HEADS, RET_DK), pos)
    vf = v.astype(jnp.float32).reshape(B, L, N_RET_HEADS, RET_DV)
    y, ret_new = retention_seq(qf.transpose(0, 2, 1, 3), kf.transpose(0, 2, 1, 3),
                               vf.transpose(0, 2, 1, 3), ret_prev.astype(jnp.float32),
                               retention_log_gamma())
    mu = jnp.mean(y, axis=-1, keepdims=True)
    var = jnp.mean(jnp.square(y - mu), axis=-1, keepdims=True)
    y = ((y - mu) * lax.rsqrt(var + EPS)).transpose(0, 2, 1, 3).reshape(B, L, RET_V)
    y = y * ret_gn_g.astype(jnp.float32)
    ret_out = (jax.nn.silu(g.astype(jnp.float32)) * y).astype(h.dtype) @ w_ret_out
    glu = u * jax.nn.sigmoid(a)
    full = jnp.concatenate([conv_prev.astype(glu.dtype), glu], axis=1)
    conv = lax.conv_general_dilated(full, conv_w[:, None, :].astype(glu.dtype), window_strides=(1,),
                                    padding='VALID', dimension_numbers=('NWC', 'WIO', 'NWC'),
                                    feature_group_count=CONV_CH) + conv_b
    conv_out = jax.nn.silu(layer_norm(conv, conv_ln_g, conv_ln_b)) @ w_conv_out
    conv_new = full[:, -CONV_STATE:, :]
    merged = jax.nn.sigmoid(gr) * ret_out + jax.nn.sigmoid(gc) * conv_out
    return merged @ w_out, conv_new, ret_new


def swiglu(h, w_gate, w_up, w_down):
    return (jax.nn.silu(h @ w_gate) * (h @ w_up)) @ w_down


def decoder_layer(x, c, pos, conv_prev, ret_prev, w_in, w_ada, b_ada, g_pre1, g_post1, g_pre2, g_post2,
                  conv_w, conv_b, conv_ln_g, conv_ln_b, w_conv_out, ret_gn_g, w_ret_out, w_out,
                  w_ffn_gate, w_ffn_up, w_ffn_down):
    mod = jax.nn.silu(c) @ w_ada + b_ada
    sh1, sc1, gt1, sh2, sc2, gt2 = [m[:, None, :] for m in jnp.split(mod, 6, axis=-1)]
    h = rms_norm(x, g_pre1) * (1.0 + sc1) + sh1
    mix, conv_new, ret_new = mixer_sublayer(h, pos, conv_prev, ret_prev, w_in, conv_w, conv_b,
                                            conv_ln_g, conv_ln_b, w_conv_out, ret_gn_g, w_ret_out, w_out)
    x = x + gt1 * rms_norm(mix, g_post1)
    h = rms_norm(x, g_pre2) * (1.0 + sc2) + sh2
    x = x + gt2 * rms_norm(swiglu(h, w_ffn_gate, w_ffn_up, w_ffn_down), g_post2)
    return x, conv_new, ret_new


def setup_inputs(seed: int = 0) -> dict:
    key = jax.random.key(seed)
    ks = jax.random.split(key, 32)
    f32 = jnp.float32

    def nrm(k, shape, scale):
        return jax.random.normal(k, shape, f32) * scale

    def gain(k, shape):
        return 1.0 + 0.01 * jax.random.normal(k, shape, f32)

    return {
        "x_prompt": nrm(ks[0], (BATCH, SEQ, D_MODEL), 1.0),
        "x_sample": nrm(ks[1], (DEC_BATCH, DEC_SEQ, D_MODEL), 1.0),
        "c_prompt": nrm(ks[2], (BATCH, D_MODEL), 1.0),
        "c_sample": nrm(ks[3], (DEC_BATCH, D_MODEL), 1.0),
        "state_ret": nrm(ks[4], (DEPTH, DEC_BATCH, N_RET_HEADS, RET_DK, RET_DV), 4.0),
        "state_conv": nrm(ks[5], (DEPTH, DEC_BATCH, CONV_STATE, CONV_CH), 0.5),
        "w_in": nrm(ks[6], (DEPTH, D_MODEL, D_IN), D_MODEL ** -0.5),
        "w_ada": nrm(ks[7], (DEPTH, D_MODEL, 6 * D_MODEL), 0.5 * D_MODEL ** -0.5),
        "b_ada": nrm(ks[8], (DEPTH, 6 * D_MODEL), 0.02),
        "g_pre1": gain(ks[9], (DEPTH, D_MODEL)),
        "g_post1": gain(ks[10], (DEPTH, D_MODEL)),
        "g_pre2": gain(ks[11], (DEPTH, D_MODEL)),
        "g_post2": gain(ks[12], (DEPTH, D_MODEL)),
        "conv_w": nrm(ks[13], (DEPTH, CONV_WIDTH, CONV_CH), CONV_WIDTH ** -0.5),
        "conv_b": nrm(ks[14], (DEPTH, CONV_CH), 0.02),
        "conv_ln_g": gain(ks[15], (DEPTH, CONV_CH)),
        "conv_ln_b": nrm(ks[16], (DEPTH, CONV_CH), 0.02),
        "w_conv_out": nrm(ks[17], (DEPTH, CONV_CH, D_MODEL), CONV_CH ** -0.5),
        "ret_gn_g": gain(ks[18], (DEPTH, RET_V)),
        "w_ret_out": nrm(ks[19], (DEPTH, RET_V, D_MODEL), RET_V ** -0.5),
        "w_out": nrm(ks[20], (DEPTH, D_MODEL, D_MODEL), D_MODEL ** -0.5),
        "w_ffn_gate": nrm(ks[21], (DEPTH, D_MODEL, D_FF), D_MODEL ** -0.5),
        "w_ffn_up": nrm(ks[22], (DEPTH, D_MODEL, D_FF), D_MODEL ** -0.5),
        "w_ffn_down": nrm(ks[23], (DEPTH, D_FF, D_MODEL), D_FF ** -0.5),
    }


def reference(x_prompt, x_sample, c_prompt, c_sample, state_ret, state_conv, w_in, w_ada, b_ada,
              g_pre1, g_post1, g_pre2, g_post2, conv_w, conv_b, conv_ln_g, conv_ln_b, w_conv_out,
              ret_gn_g, w_ret_out, w_out, w_ffn_gate, w_ffn_up, w_ffn_down):
    pos_p = jnp.arange(SEQ, dtype=jnp.float32)
    pos_s = PAST_LEN + jnp.arange(DEC_SEQ, dtype=jnp.float32)
    xp, xs = x_prompt, x_sample
    ret_p_list, ret_s_list, conv_p_list, conv_s_list = [], [], [], []
    for l in range(DEPTH):
        layer_w = (w_in[l], w_ada[l], b_ada[l], g_pre1[l], g_post1[l], g_pre2[l], g_post2[l],
                   conv_w[l], conv_b[l], conv_ln_g[l], conv_ln_b[l], w_conv_out[l], ret_gn_g[l],
                   w_ret_out[l], w_out[l], w_ffn_gate[l], w_ffn_up[l], w_ffn_down[l])
        conv0 = jnp.zeros((BATCH, CONV_STATE, CONV_CH), xp.dtype)
        ret0 = jnp.zeros((BATCH, N_RET_HEADS, RET_DK, RET_DV), jnp.float32)
        xp, conv_p, ret_p = decoder_layer(xp, c_prompt, pos_p, conv0, ret0, *layer_w)
        xs, conv_s, ret_s = decoder_layer(xs, c_sample, pos_s, state_conv[l], state_ret[l], *layer_w)
        ret_p_list.append(ret_p.astype(x_prompt.dtype))
        ret_s_list.append(ret_s.astype(state_ret.dtype))
        conv_p_list.append(conv_p.astype(x_prompt.dtype))
        conv_s_list.append(conv_s.astype(state_conv.dtype))
    ret_state_prompt = jnp.stack(ret_p_list, axis=0)
    ret_state_sample = jnp.stack(ret_s_list, axis=0)
    conv_state_prompt = jnp.stack(conv_p_list, axis=0)
    conv_state_sample = jnp.stack(conv_s_list, axis=0)
    return (xp, xs, ret_state_prompt, ret_state_sample, conv_state_prompt, conv_state_sample)
```

```python
from contextlib import ExitStack
import numpy as np
import concourse.bass as bass
import concourse.mybir as mybir
from concourse.bass_utils import run_bass_kernel_spmd

F32 = mybir.dt.float32
BF16 = mybir.dt.bfloat16
AF = mybir.ActivationFunctionType
ALU = mybir.AluOpType
_DT_SIZE = {F32: 4, BF16: 2}

D = 1024
NH = 4
EPS = 1e-6
NS = 7
NPP = 8 * 5 + 16 + 8 * 31


class View:
    def __init__(self, tt, ap):
        self.tt = tt
        self.ap = ap

    def m(self, f):
        return View(self.tt, f(self.ap))

    def __getitem__(self, k):
        return View(self.tt, self.ap[k])


class TT:
    def __init__(self, name, space, lo, hi, handle):
        self.name = name
        self.space = space
        self.lo = lo
        self.hi = hi
        self.h = handle
        self.dsem = None
        self.dcount = 0

    def __getitem__(self, k):
        return View(self, self.h[k])


class Op:
    __slots__ = ("eng", "fn", "waits", "signal", "sigval", "is_dma", "dtok", "idx", "lbl")


class FW:
    ALLENG = ("pe", "act", "dve", "pool", "sp")

    def __init__(self, nc):
        self.nc = nc
        self.ops = {e: [] for e in self.ALLENG}
        self.recs = []
        self.waited = {e: {} for e in self.ALLENG}
        self.dma_sems = []
        self.uid = 0
        self.ARENA = 204800
        arena = nc.alloc_sbuf_tensor("arena", [128, self.ARENA // 4], F32)
        self.base = nc.lookup_mloc(arena).addr
        self.banks = []
        for i in range(8):
            h = nc.alloc_psum_tensor(f"psb{i}", [128, 512], F32)
            self.banks.append(TT(f"psb{i}", "ps", i * 2048, (i + 1) * 2048, h))
        self.held = set()
        self.rr = 0

    def sb(self, name, shape, dtype, off):
        nbytes = int(np.prod(shape[1:])) * _DT_SIZE[dtype]
        assert off % 4 == 0 and off + nbytes <= self.ARENA, (name, off, nbytes)
        self.uid += 1
        h = self.nc.alloc_sbuf_tensor_at(f"{name}_{self.uid}", list(shape), dtype,
                                         offset=self.base + off)
        return TT(name, "sb", off, off + nbytes, h)

    def ps(self, hold=False):
        for _ in range(8):
            b = self.banks[self.rr]
            self.rr = (self.rr + 1) % 8
            if b not in self.held:
                if hold:
                    self.held.add(b)
                return b
        raise RuntimeError("no psum bank")

    def release(self, b):
        self.held.discard(b)

    def _deps(self, reads, writes):
        deps = []
        for t in reads:
            for r in self.recs:
                if r[3] == "w" and r[0] == t.space and r[1] < t.hi and t.lo < r[2]:
                    deps.append(r[4])
        for t in writes:
            for r in self.recs:
                if r[0] == t.space and r[1] < t.hi and t.lo < r[2]:
                    deps.append(r[4])
        return deps

    def _record(self, tok, reads, writes):
        for t in writes:
            self.recs = [r for r in self.recs
                         if not (r[0] == t.space and t.lo <= r[1] and r[2] <= t.hi)]
            self.recs.append([t.space, t.lo, t.hi, "w", tok])
        for t in reads:
            if tok[0] == "c":
                found = False
                for r in self.recs:
                    if (r[3] == "r" and r[0] == t.space and r[1] == t.lo and r[2] == t.hi
                            and r[4][0] == "c" and r[4][1].eng == tok[1].eng):
                        r[4] = tok
                        found = True
                        break
                if found:
                    continue
            self.recs.append([t.space, t.lo, t.hi, "r", tok])

    def _mk_waits(self, eng, deps):
        best = {}
        for d in deps:
            if d[0] == "c":
                o = d[1]
                if o.eng == eng and eng == "pe":
                    continue
                key = ("c", o.eng)
                if key not in best or best[key][1].idx < o.idx:
                    best[key] = d
            else:
                key = ("d", id(d[1]))
                if key not in best or best[key][2] < d[2]:
                    best[key] = d
        waits = []
        w = self.waited[eng]
        for key, d in best.items():
            v = d[1].idx if d[0] == "c" else d[2]
            if w.get(key, -1) >= v:
                continue
            w[key] = v
            if d[0] == "c":
                d[1].signal = True
            waits.append(d)
        return waits

    def _newop(self, eng, fn):
        op = Op()
        op.eng = eng
        op.fn = fn
        op.signal = False
        op.sigval = None
        op.is_dma = False
        op.dtok = None
        op.idx = len(self.ops[eng])
        op.lbl = getattr(self, "lbl", "")
        return op

    def emit(self, eng, fn, reads=(), writes=()):
        op = self._newop(eng, fn)
        op.waits = self._mk_waits(eng, self._deps(reads, writes))
        self.ops[eng].append(op)
        self._record(("c", op), reads, writes)
        return op

    def dma(self, queue, fn, semtt, reads=(), writes=()):
        op = self._newop(queue, fn)
        op.is_dma = True
        if semtt.dsem is None:
            semtt.dsem = True
            self.dma_sems.append(semtt)
        semtt.dcount += 16
        tok = ("d", semtt, semtt.dcount)
        op.dtok = tok
        op.waits = self._mk_waits(queue, self._deps(reads, writes))
        self.ops[queue].append(op)
        self._record(tok, reads, writes)
        return op

    def final_wait(self, queue="sp"):
        op = self._newop(queue, None)
        op.waits = self._mk_waits(queue, [r[4] for r in self.recs])
        self.ops[queue].append(op)

    def generate(self, stack):
        nc = self.nc
        esem = {e: stack.enter_context(nc.semaphore(f"s_{e}")) for e in self.ALLENG}
        for i, t in enumerate(self.dma_sems):
            t.dsem = stack.enter_context(nc.semaphore(f"d_{i}"))
        for e in self.ALLENG:
            c = 0
            for op in self.ops[e]:
                if op.signal:
                    c += 1
                    op.sigval = c
        block = stack.enter_context(nc.Block())
        engmap = {"pe": block.tensor, "act": block.scalar, "dve": block.vector,
                  "pool": block.gpsimd, "sp": block.sync}

        def run(e):
            def body(engine):
                for op in self.ops[e]:
                    for d in op.waits:
                        if d[0] == "c":
                            engine.wait_ge(esem[d[1].eng], d[1].sigval)
                        else:
                            engine.wait_ge(d[1].dsem, d[2])
                    if op.fn is None:
                        continue
                    ins = op.fn(engine)
                    if op.is_dma:
                        ins.then_inc(op.dtok[1].dsem, 16)
                    elif op.signal:
                        ins.then_inc(esem[e], 1)
            return body

        for e in self.ALLENG:
            engmap[e](run(e))


def _tts(*vs):
    return [v.tt for v in vs if isinstance(v, View)]


def _a(v):
    return v.ap if isinstance(v, View) else v


class Builder:
    def __init__(self):
        self.nc = bass.Bass("TRN2", target_bir_lowering=False)
        self.fw = FW(self.nc)
        self.D = {}

    def din(self, name, shape):
        self.D[name] = self.nc.dram_tensor(name, list(shape), F32, kind="ExternalInput").ap()

    def dout(self, name, shape):
        self.D[name] = self.nc.dram_tensor(name, list(shape), F32, kind="ExternalOutput").ap()

    def mm(self, out, lhsT, rhs, start, stop):
        self.fw.emit("pe", lambda e: e.matmul(out.ap, lhsT=lhsT.ap, rhs=rhs.ap, start=start, stop=stop),
                     reads=_tts(lhsT, rhs), writes=[out.tt])

    def tr(self, out, in_, ident):
        self.fw.emit("pe", lambda e: e.transpose(out.ap, in_.ap, ident.ap),
                     reads=_tts(in_, ident), writes=[out.tt])

    def act(self, out, in_, func, bias=None, scale=None, accum=None):
        kw = {}
        if bias is not None:
            kw["bias"] = _a(bias)
        if scale is not None:
            kw["scale"] = _a(scale)
        if accum is not None:
            kw["accum_out"] = accum.ap
        self.fw.emit("act", lambda e: e.activation(out=out.ap, in_=in_.ap, func=func, **kw),
                     reads=_tts(in_, bias, scale), writes=_tts(out, accum))

    def tt(self, eng, out, in0, in1, op):
        self.fw.emit(eng, lambda e: e.tensor_tensor(out=out.ap, in0=in0.ap, in1=in1.ap, op=op),
                     reads=_tts(in0, in1), writes=[out.tt])

    def ts(self, eng, out, in0, s1, s2, op0, op1=None):
        if op1 is None:
            f = lambda e: e.tensor_scalar(out=out.ap, in0=in0.ap, scalar1=_a(s1), scalar2=None, op0=op0)
        else:
            f = lambda e: e.tensor_scalar(out=out.ap, in0=in0.ap, scalar1=_a(s1), scalar2=_a(s2),
                                          op0=op0, op1=op1)
        self.fw.emit(eng, f, reads=_tts(in0, s1, s2), writes=[out.tt])

    def stt(self, eng, out, in0, scalar, in1, op0, op1):
        self.fw.emit(eng, lambda e: e.scalar_tensor_tensor(out=out.ap, in0=in0.ap, scalar=_a(scalar),
                                                           in1=in1.ap, op0=op0, op1=op1),
                     reads=_tts(in0, scalar, in1), writes=[out.tt])

    def cp(self, eng, out, in_):
        self.fw.emit(eng, lambda e: e.tensor_copy(out=out.ap, in_=in_.ap), reads=[in_.tt], writes=[out.tt])

    def memset(self, eng, out, val):
        self.fw.emit(eng, lambda e: e.memset(out.ap, val), writes=[out.tt])

    def recip(self, out, in_):
        self.fw.emit("dve", lambda e: e.reciprocal(out=out.ap, in_=in_.ap), reads=[in_.tt], writes=[out.tt])

    def dma(self, queue, out, in_, xr=(), xw=()):
        xr, xw = list(xr), list(xw)
        if isinstance(out, View) and isinstance(in_, View):
            raise NotImplementedError
        if isinstance(out, View):
            self.fw.dma(queue, lambda e: e.dma_start(out=out.ap, in_=in_), out.tt, reads=xr, writes=[out.tt] + xw)
        elif isinstance(in_, View):
            self.fw.dma(queue, lambda e: e.dma_start(out=out, in_=in_.ap), in_.tt, reads=[in_.tt] + xr, writes=xw)
        else:
            self.fw.dma(queue, lambda e: e.dma_start(out=out, in_=in_), self.dummy, reads=xr,
                        writes=[self.dummy] + xw)

    def rsqrt(self, out, in_, scale, CS):
        self.act(out, in_, AF.Sqrt, bias=self.epsc[0:CS, 0:1], scale=scale)
        self.recip(out, out)


class G:
    pass


def bf(v):
    return v.m(lambda a: a.bitcast(BF16))


def build_program():
    B = Builder()
    fw = B.fw
    DD = B.D
    for n, s in [("xp", [2048, D]), ("xs", [16, D]), ("cT", [128, 8 * 17]), ("sret", [16, 4, 256, 512]),
                 ("sconv", [16, 30, D]), ("w_in", [D, 10240]), ("w_ada", [D, 6144]), ("b_ada", [1, 6144]),
                 ("w_conv_out", [D, D]), ("w_ret_out", [2048, D]), ("w_out", [D, D]),
                 ("w_gate", [D, 2816]), ("w_up", [D, 2816]), ("w_down", [2816, D]),
                 ("pp", [128, NPP]), ("gpost", [2, D]), ("tabk", [128, 2, 2048]),
                 ("tabq", [4, 128, 2, 2048]), ("tabs", [128, 10, 16]), ("dm", [128, 512]),
                 ("kdec", [128, 4]), ("ident", [128, 128]), ("sel", [17, 128]), ("diag16", [128, 256])]:
        B.din(n, s)
    for n, s in [("yp", [2048, D]), ("ys", [16, D]), ("rsp", [4, 256, 512]), ("rss", [16, 4, 256, 512]),
                 ("csp", [30, D]), ("css", [16, 30, D])]:
        B.dout(n, s)

    NBP = 46
    wscr = B.nc.dram_tensor("wscr", [NBP, 128, 4096], BF16, kind="Internal").ap()
    gscr = B.nc.dram_tensor("gscr", [2, 16, D], F32, kind="Internal").ap()
    wscr_tt = [TT(f"wscr{b}", "dram", b, b + 1, None) for b in range(NBP)]
    gscr_tt = [TT(f"gscr{b}", "dram", 1000 + b, 1001 + b, None) for b in range(2)]
    off = 0
    slots = []
    for i in range(NS):
        slots.append(fw.sb(f"slot{i}", [128, 8, 512], BF16, off))
        off += 8192
    S32 = []
    for i in range(8):
        S32.append(fw.sb(f"S32_{i}", [128, 512], F32, off))
        off += 2048
    S16 = []
    for i in range(8):
        S16.append(fw.sb(f"S16_{i}", [128, 512], BF16, off))
        off += 1024
    coff = [off]

    def calloc(name, shape, dt):
        n = int(np.prod(shape[1:])) * _DT_SIZE[dt]
        t = fw.sb(name, shape, dt, coff[0])
        coff[0] += (n + 31) // 32 * 32
        return t

    identb = calloc("identb", [128, 128], BF16)
    identf = calloc("identf", [128, 128], F32)
    onesb = calloc("onesb", [128, 128], BF16)
    dm = calloc("dm", [128, 512], F32)
    G1bc = calloc("G1bc", [128, D], F32)
    G2bc = calloc("G2bc", [128, D], F32)
    pp = calloc("pp", [128, NPP], F32)
    modT = calloc("modT", [128, 48, 17], F32)
    A1T = calloc("A1T", [128, 8, 17], F32)
    A2T = calloc("A2T", [128, 8, 17], F32)
    kdec = calloc("kdec", [128, 4], F32)
    diag16 = calloc("diag16", [128, 256], F32)
    sel = calloc("sel", [17, 128], F32)
    B.epsc = calloc("epsc", [128, 1], F32)
    B.dummy = calloc("dummy", [128, 1], F32)
    scT = calloc("scT", [128, 8 * 17], BF16)
    cvs_p = calloc("cvs_p", [128, 8, 16], F32)
    tabs = calloc("tabs", [128, 10, 16], F32)
    halo = calloc("halo", [128, 8, 32], BF16)
    assert coff[0] - off <= 20480, coff[0] - off
    off += 20480
    R0 = off
    RSZ = fw.ARENA - R0
    assert RSZ >= 102400, RSZ

    gpre1 = pp[:, 0:8]
    gpre2 = pp[:, 8:16]
    convb = pp[:, 16:24]
    lng = pp[:, 24:32]
    lnb = pp[:, 32:40]
    gng = pp[:, 40:56]
    cwT = pp[:, 56:56 + 248].m(lambda a: a.rearrange("p (c w) -> p c w", w=31))
    Bv1T = modT[:, 0:8, :]
    Bv2T = modT[:, 24:32, :]

    lg = np.log(1.0 - np.exp(np.linspace(np.log(1.0 / 32.0), np.log(1.0 / 512.0), NH)))
    gam = np.exp(lg)
    gam128 = np.exp(lg * 128.0)

    def mk_group(kind, tile):
        g = G()
        g.kind = kind
        g.tile = tile
        P = kind == "p"
        g.T = 512 if P else 16
        g.CS = 128 if P else 16
        g.NJ = 4 if P else 1
        T, CS, NJ = g.T, g.CS, g.NJ
        o = [R0]

        def al(name, shape, dt, at=None):
            if at is not None:
                o[0] = at
            n = int(np.prod(shape[1:])) * _DT_SIZE[dt]
            t = fw.sb(f"{kind}{name}", shape, dt, o[0])
            o[0] += (n + 31) // 32 * 32
            return t

        g.xs = None
        g.hT = [al(f"hT{k}", [128, T], BF16) for k in range(8)]
        g.cpart = [al(f"cp{k}", [128, T], BF16) for k in range(8)]
        PH = R0 + 16384 if P else o[0]
        g.sgc = [al(f"sgc{k}", [128, T], BF16, at=PH if k == 0 else None) for k in range(8)]
        g.siga = [al(f"siga{k}", [128, T], BF16) for k in range(4)]
        g.glu = [al(f"glu{k}", [128, 544 if P else 16], BF16 if P else F32) for k in range(8)]
        g.diag = [al(f"diag{k}", [128, 31, 128], BF16) for k in range(2)] if P else None
        g.cvT = [al(f"cvT{k}", [128, T], BF16) for k in range(8)]
        g.sq = [al(f"sq{k}", [128, T], BF16) for k in range(2)]
        g.mean = al("mean", [128, T], F32)
        g.rstd = al("rstd", [128, T], F32)
        g.msq = al("msq", [128, T], F32)
        g.lntmp = al("lntmp", [128, T], F32)
        g.lntmpb = al("lntmpb", [128, T], F32)
        g.lnT = [al(f"lnT{k}", [128, T], BF16) for k in range(8)]
        e_ln = o[0]
        g.gtail = al("gtail", [128, 8, 32], F32)
        g.ctail = al("ctail", [32, D], F32)
        if not P:
            g.strow = [al(f"strow{k}", [120, D], F32) for k in range(1)]
            g.cvs = al("cvs", [128, 8, 16], F32)
            g.cvtmp = al("cvtmp", [128, 120], F32)
        e1 = o[0]
        g.rT = [al(f"rT{k}", [128, 4, T], BF16, at=(PH if P else e_ln) if k == 0 else None) for k in range(4)]
        g.qk = [al(f"qk{k}", [128, T], BF16, at=(PH + 16384 if (P and k == 0) else None)) for k in range(16)]
        g.tmp = [al(f"tmp{k}", [128, T], F32, at=(PH + 68608 if (P and k == 0) else None)) for k in range(4)]
        g.tabk = al("tabk", [128, 2, T], F32)
        g.tabq = al("tabq", [128, 2, T], F32)
        HG = 2 if P else 1
        g.vv = [[al(f"v{hh}_{k}", [CS, 512], BF16, at=(PH + 36864 if (P and hh == 0 and k == 0) else None))
                 for k in range(NJ)] for hh in range(HG)]
        g.sgg = [[al(f"sg{hh}_{k}", [CS, 512], BF16) for k in range(NJ)] for hh in range(HG)]
        g.khh = [al(f"khat{hh}", [CS, NJ, 256], BF16) for hh in range(HG)]
        g.Pm = [al(f"Pm{k}", [128, 128], BF16) for k in range(2)]
        g.rn = [al(f"rn{k}", [CS, 512], BF16) for k in range(2)]
        g.r = [al(f"r{k}", [CS, 512], BF16) for k in range(6 if P else 2)]
        g.st6s = [al(f"st6_{k}", [128, 6], F32) for k in range(4)]
        g.mvs = [al(f"mv_{k}", [128, 2], F32) for k in range(4)]
        g.gsds = [al(f"gsd_{k}", [128, 1], F32) for k in range(4)]
        g.gnbs = [al(f"gnb_{k}", [128, 1], F32) for k in range(4)]
        g.st6 = al("st6", [128, 6], F32)
        g.mv = al("mv", [128, 2], F32)
        g.gsd = al("gsd", [128, 1], F32)
        g.gnb = al("gnb", [128, 1], F32)
        if not P:
            g.st = None
            g.km = [al(f"km{k}", [16, 256], BF16) for k in range(2)]
            g.qTm = al("qTm", [128, 2, 16, 16], BF16)
        e2 = max(o[0], PH + 84992) if P else o[0]
        g.sgr = [al(f"sgr{k}", [128, T], BF16, at=(PH + 16384 if P else e2) if k == 0 else None)
                 for k in range(8)]
        g.merged = [al(f"mg{k}", [128, T], BF16) for k in range(8)]
        g.mix = [al(f"mix{k}", [CS, D], F32) for k in range(NJ)]
        g.mtmp = al("mtmp", [128, T], F32)
        g.junk = al("junk", [CS, 512], BF16)
        g.ssm = al("ssm", [128, 8], F32)
        g.ss2 = al("ss2", [128, 4], F32)
        e3 = o[0]
        g.xn = [al(f"xn{k}", [CS, D], BF16, at=PH if k == 0 else None) for k in range(NJ)]
        g.aT = [al(f"aT{k}", [128, T], BF16) for k in range(22)]
        g.sgt = [al(f"sgt{k}", [128, T], BF16) for k in range(4)]
        g.ffo = [al(f"ffo{k}", [CS, D], F32) for k in range(NJ)]
        g.ssj = [al(f"ssj{k}", [128, 1], F32) for k in range(NJ)]
        g.rsj = [al(f"rsj{k}", [128, 1], F32) for k in range(NJ)]
        g.ntmp = al("ntmp", [128, T], F32)
        e4 = o[0]
        if P:
            g.xs1 = [al(f"xs1_{k}", [128, D], F32, at=PH + 53248 if k == 0 else None) for k in range(4)]
            g.xs0 = [al(f"xs0_{k}", [128, D], F32) for k in range(4)]
            e4 = max(e4, o[0])
        if not P:
            o[0] = max(e1, e2, e3, e4)
            g.xs0 = g.xs1 = [al("xss", [16, D], F32)]
            g.G1s = al("G1s", [16, D], F32)
            g.G2s = al("G2s", [16, D], F32)
            e4 = o[0]
        assert max(e1, e2, e3, e4) <= fw.ARENA, (e1, e2, e3, e4)
        g.end = max(e1, e2, e3, e4)
        return g

    gs = mk_group("s", 0)
    gp = mk_group("p", 0)

    so = gs.end
    modrows = fw.sb("modrows", [17, 6144], F32, so)
    bada = fw.sb("bada", [17, 6144], F32, so + 24576)
    gpostbc = fw.sb("gpostbc", [128, 2, D], F32, so + 49152)
    assert so + 49152 + 8192 + 1024 <= fw.ARENA
    cTf = fw.sb("cTf", [128, 8 * 17], F32, so + 57344)
    NST = 10
    gs.st = [fw.sb(f"st{k}", [128, 2, 512], F32, so + k * 4096) for k in range(NST)]
    gs.stb = [fw.sb(f"stb{k}", [128, 2, 512], BF16, so + NST * 4096 + k * 2048) for k in range(4)]
    st_issued = [0]

    def st_load_upto(n):
        while st_issued[0] < min(n, 64):
            i = st_issued[0]
            B.dma("sp", gs.st[i % NST][:, :, :],
                  DD["sret"][i % 16, i // 16].rearrange("(c p) v -> p c v", p=128))
            st_issued[0] += 1

    blocks = []

    def wblk(name, r0, nk, c0, ncols):
        blocks.append((DD[name][r0:r0 + nk * 128, c0:c0 + ncols], nk, ncols))
        return len(blocks) - 1

    issued = [0]

    blk_pb = {}

    def issue_upto(n):
        while issued[0] < min(n, len(blocks)):
            b = issued[0]
            src, nk, ncols = blocks[b]
            slot = slots[b % NS]
            pinfo = blk_pb.get(b)
            if pinfo is None or pinfo[0] == 0:
                B.dma("pool", slot[:, 0:nk, 0:ncols], src.rearrange("(k p) n -> p k n", p=128))
            else:
                B.dma("pool", slot[:, :, :].m(lambda a: a.rearrange("p k n -> p (k n)")), wscr[pinfo[1]],
                      xr=[wscr_tt[pinfo[1]]])
            issued[0] += 1

    def cache_block(b):
        pinfo = blk_pb.get(b)
        slot = slots[b % NS]
        if pinfo is not None and pinfo[0] == 0:
            B.dma("sp", wscr[pinfo[1]], slot[:, :, :].m(lambda a: a.rearrange("p k n -> p (k n)")),
                  xw=[wscr_tt[pinfo[1]]])

    def use_block(b, cache=True, keep=0):
        issue_upto(b + NS - keep)
        slot = slots[b % NS]
        pinfo = blk_pb.get(b)
        if cache and pinfo is not None and pinfo[0] == 0:
            B.dma("sp", wscr[pinfo[1]], slot[:, :, :].m(lambda a: a.rearrange("p k n -> p (k n)")),
                  xw=[wscr_tt[pinfo[1]]])
        return slot

    sched = {"ada": [wblk("w_ada", 0, 8, c * 512, 512) for c in range(12)]}
    passes = [("p", t) for t in range(4)] + [("s", 0)]
    for pi, _ in enumerate(passes):
        s = {}
        nb0 = len(blocks)
        s["gc"] = [wblk("w_in", 0, 8, 9216 + b * 512, 512) for b in range(2)]
        s["au"] = []
        for b in range(2):
            s["au"].append(wblk("w_in", 0, 8, 7168 + b * 512, 512))
            s["au"].append(wblk("w_in", 0, 8, 6144 + b * 512, 512))
        s["q"] = [wblk("w_in", 0, 8, b * 512, 512) for b in range(2)]
        s["k"] = [wblk("w_in", 0, 8, 1024 + b * 512, 512) for b in range(2)]
        s["wc"] = [wblk("w_conv_out", 0, 8, b * 512, 512) for b in range(2)]
        s["vg"] = []
        for h in range(4):
            s["vg"].append(wblk("w_in", 0, 8, 2048 + h * 512, 512))
            s["vg"].append(wblk("w_in", 0, 8, 4096 + h * 512, 512))
        s["gr"] = [wblk("w_in", 0, 8, 8192 + b * 512, 512) for b in range(2)]
        s["wr"] = [wblk("w_ret_out", rh * 1024, 8, ch * 512, 512) for ch in range(2) for rh in range(2)]
        s["wo"] = [wblk("w_out", 0, 8, b * 512, 512) for b in range(2)]
        s["gu"] = []
        for i in range(6):
            nc_ = 512 if i < 5 else 256
            s["gu"].append(wblk("w_gate", 0, 8, i * 512, nc_))
            s["gu"].append(wblk("w_up", 0, 8, i * 512, nc_))
        s["wd"] = [wblk("w_down", rb * 1024, 8 if rb < 2 else 6, ch * 512, 512)
                   for ch in range(2) for rb in range(3)]
        sched[pi] = s
        assert len(blocks) - nb0 == NBP
        for i in range(NBP):
            blk_pb[nb0 + i] = (pi, i)

    sp = "sp"
    B.dma(sp, pp[:, :], DD["pp"])
    B.dma(sp, identf[:, :], DD["ident"])
    B.dma("pool", identb[:, :], DD["ident"])
    B.dma(sp, dm[:, :], DD["dm"])
    B.dma(sp, kdec[:, :], DD["kdec"])
    B.dma(sp, diag16[:, :], DD["diag16"])
    B.dma(sp, sel[:, :], DD["sel"])
    B.dma(sp, cTf[:, :], DD["cT"])
    B.dma(sp, tabs[:, :, :], DD["tabs"])
    B.dma(sp, bada[:, :], DD["b_ada"].partition_broadcast(17))
    for i in range(2):
        B.dma(sp, gpostbc[:, i, :], DD["gpost"][i:i + 1, :].partition_broadcast(128))
    B.memset("pool", onesb[:, :], 1.0)
    B.memset("pool", halo[:, :, :], 0.0)
    B.memset("pool", B.epsc[:, :], EPS)
    for i in range(8):
        B.memset("pool", S32[i][:, :], 0.0)
        B.memset("pool", S16[i][:, :], 0.0)
    issue_upto(NS - 1)
    B.dma(sp, DD["css"][:, 0:29, :], DD["sconv"][:, 1:30, :])

    for rt in range(4):
        strow = gs.strow[0]
        B.dma(sp, strow[:, :], DD["sconv"][rt * 4:(rt + 1) * 4, :, :].rearrange("s w d -> (s w) d"))
        for c in range(8):
            bank = fw.ps()
            B.tr(bank[:, 0:120], strow[:, c * 128:(c + 1) * 128], identf[0:120, 0:120])
            B.tt("dve", gs.cvtmp[:, :].m(lambda a: a.rearrange("p (s w) -> p s w", w=30)),
                 bank[:, 0:120].m(lambda a: a.rearrange("p (s w) -> p s w", w=30)),
                 cwT.m(lambda a: a[:, c, 0:30].unsqueeze(1).broadcast_to([128, 4, 30])), ALU.mult)
            B.fw.emit("dve", lambda e, c=c, rt=rt: e.reduce_sum(
                out=cvs_p[:, c, rt * 4:(rt + 1) * 4].ap,
                in_=gs.cvtmp[:, :].ap.rearrange("p (s w) -> p s w", w=30),
                axis=mybir.AxisListType.X), reads=[gs.cvtmp], writes=[cvs_p])
    B.act(scT[:, :], cTf[:, :], AF.Silu)
    for c, b in enumerate(sched["ada"]):
        slot = use_block(b)
        bank = fw.ps()
        for k in range(8):
            B.mm(bank[0:17, :], scT[:, k * 17:(k + 1) * 17], slot[:, k, :], k == 0, k == 7)
        B.tt("dve", modrows[:, c * 512:(c + 1) * 512], bank[0:17, :], bada[:, c * 512:(c + 1) * 512], ALU.add)
    for half in range(2):
        bank = fw.ps()
        for q in range(24):
            cq = half * 24 + q
            B.tr(bank[:, q * 17:(q + 1) * 17], modrows[0:17, cq * 128:(cq + 1) * 128], identf[0:17, 0:17])
        B.cp("dve", modT[:, half * 24:(half + 1) * 24, :],
             bank[:, 0:408].m(lambda a: a.rearrange("p (q s) -> p q s", s=17)))
    for (AT, c0, gp_) in ((A1T, 8, gpre1), (A2T, 32, gpre2)):
        B.ts("dve", AT[:, :, :], modT[:, c0:c0 + 8, :], 1.0, None, ALU.add)
        B.tt("dve", AT[:, :, :], AT[:, :, :],
             gp_.m(lambda a: a.unsqueeze(2).broadcast_to([128, 8, 17])), ALU.mult)
    for (Gbc, Gs, c0, gi) in ((G1bc, gs.G1s, 2048, 0), (G2bc, gs.G2s, 5120, 1)):
        for ch in range(2):
            bank = fw.ps()
            B.mm(bank[:, :], sel[:, :], modrows[:, c0 + ch * 512:c0 + (ch + 1) * 512], True, True)
            B.tt("dve", Gbc[:, ch * 512:(ch + 1) * 512], bank[:, :], gpostbc[:, gi, ch * 512:(ch + 1) * 512],
                 ALU.mult)
        B.tt("dve", Gs[:, :], modrows[0:16, c0:c0 + 1024], gpostbc[0:16, gi, :], ALU.mult)
        B.dma(sp, gscr[gi], Gs[:, :], xw=[gscr_tt[gi]])

    def norm_to_hT(g, AT, BT, xs):
        T, CS, NJ = g.T, g.CS, g.NJ
        tbk = [fw.ps(hold=True) for _ in range(4)]
        tbb = [bf(b_[:, :]) for b_ in tbk]
        for j in range(NJ):
            B.act(g.xn[j][:, :], xs[j][0:CS, :], AF.Square, accum=g.ssj[j][0:CS, :])
            B.rsqrt(g.rsj[j][0:CS, :], g.ssj[j][0:CS, :], 1.0 / D, CS)
            B.ts("dve", g.xn[j][:, :], xs[j][0:CS, :], g.rsj[j][0:CS, 0:1], None, ALU.mult)
            for k in range(8):
                c0 = (k % 2) * 512 + j * CS
                B.tr(tbb[k // 2].m(lambda a: a[:, c0:c0 + CS]), g.xn[j][:, k * 128:(k + 1) * 128],
                     identb[0:CS, 0:CS])
        for k in range(8):
            c0 = (k % 2) * 512
            src = tbb[k // 2].m(lambda a: a[:, c0:c0 + T])
            if g.kind == "p":
                B.act(g.hT[k][:, :], src, AF.Identity, scale=AT[:, k, 16:17], bias=BT[:, k, 16:17])
            else:
                B.tt("dve", g.ntmp[:, :], src, AT[:, k, 0:16], ALU.mult)
                B.tt("dve", g.hT[k][:, :], g.ntmp[:, :], BT[:, k, 0:16], ALU.add)
        for b_ in tbk:
            fw.release(b_)

    def proj_fm_kouter(g, slot, nk, rhs, nm, evac):
        banks = [fw.ps(hold=True) for _ in range(nm)]
        for k in range(nk):
            for m in range(nm):
                B.mm(banks[m][:, 0:g.T], slot[:, k, m * 128:(m + 1) * 128], rhs(k), k == 0, k == nk - 1)
        for m in range(nm):
            evac(m, banks[m][:, 0:g.T])
            fw.release(banks[m])

    def proj_fm(g, slot, nk, rhs, nm, evac, banks=None, first=True, last=True):
        for m in range(nm):
            bank = banks[m] if banks is not None else fw.ps()
            for k in range(nk):
                B.mm(bank[:, 0:g.T], slot[:, k, m * 128:(m + 1) * 128], rhs(k),
                     first and k == 0, last and k == nk - 1)
            if last and evac is not None:
                evac(m, bank[:, 0:g.T])

    def proj_tm(g, slot, nk, lhs, ncols, evac, banks=None, first=True, last=True):
        CS = g.CS
        for j in range(g.NJ):
            bank = banks[j] if banks is not None else fw.ps()
            for k in range(nk):
                B.mm(bank[0:CS, 0:ncols], lhs(k, j), slot[:, k, 0:ncols],
                     first and k == 0, last and k == nk - 1)
            if last and evac is not None:
                evac(j, bank[0:CS, 0:ncols])

    def resid(g, src, ssx, Gp, Gs_, after=None):
        CS, NJ = g.CS, g.NJ
        Gv = Gp[:, :] if g.kind == "p" else Gs_[:, :]
        for j in range(NJ):
            B.act(g.xn[j][:, :], src[j][:, :], AF.Square, accum=g.ssj[j][0:CS, :])
            B.rsqrt(g.rsj[j][0:CS, :], g.ssj[j][0:CS, :], 1.0 / D, CS)
            B.stt("dve", src[j][:, :], src[j][:, :], g.rsj[j][0:CS, 0:1], Gv, ALU.mult, ALU.mult)
            B.tt("dve", g.xs1[j][0:CS, :], g.xs1[j][0:CS, :], src[j][:, :], ALU.add)
            if after is not None:
                after(j)

    def rotary(g, bA, bB, cos, sin, outA, outB):
        t = g.tmp
        B.tt("dve", t[0][:, :], bA, cos, ALU.mult)
        B.tt("dve", t[1][:, :], bB, sin, ALU.mult)
        B.tt("pool", outA[:, :], t[0][:, :], t[1][:, :], ALU.subtract)
        B.tt("dve", t[2][:, :], bB, cos, ALU.mult)
        B.tt("dve", t[3][:, :], bA, sin, ALU.mult)
        B.tt("pool", outB[:, :], t[2][:, :], t[3][:, :], ALU.add)

    def gn_gate(g, ybank, sgv, rn, r):
        CS = g.CS
        B.fw.emit("dve", lambda e: e.bn_stats(out=g.st6[0:CS, :].ap, in_=ybank.ap),
                  reads=[ybank.tt], writes=[g.st6])
        B.fw.emit("dve", lambda e: e.bn_aggr(out=g.mv[0:CS, :].ap, in_=g.st6[0:CS, :].ap),
                  reads=[g.st6], writes=[g.mv])
        B.rsqrt(g.gsd[0:CS, :], g.mv[0:CS, 1:2], 1.0, CS)
        B.stt("dve", g.gnb[0:CS, :], g.mv[0:CS, 0:1], -1.0, g.gsd[0:CS, :], ALU.mult, ALU.mult)
        B.act(rn[:, :], ybank, AF.Identity, scale=g.gsd[0:CS, 0:1], bias=g.gnb[0:CS, 0:1])
        B.tt("pool", r[:, :], rn[:, :], sgv, ALU.mult)

    def sq_evac(g, dst, bankv, acc):
        B.act(dst, bankv, AF.Copy)

    class _Stop(Exception):
        pass

    def front(g, t):
        if g.kind == "p":
            for j in range(4):
                B.dma(sp, g.xs0[j][:, :], DD["xp"][t * 512 + j * 128:t * 512 + (j + 1) * 128, :])
        else:
            B.dma(sp, g.xs0[0][:, :], DD["xs"])
            B.dma(sp, g.G1s[:, :], gscr[0], xr=[gscr_tt[0]])
            B.dma(sp, g.G2s[:, :], gscr[1], xr=[gscr_tt[1]])
            st_load_upto(NST - 1)
        norm_to_hT(g, A1T, Bv1T, g.xs0)

    def run_pass(pi, g):
        try:
            run_pass_(pi, g)
        except _Stop:
            pass

    def run_pass_(pi, g):
        import os
        kph = float(os.environ.get("KPH", "99")) if pi == int(os.environ.get("KSTOP", "99")) - 1 else 99
        P = g.kind == "p"
        T, CS, NJ = g.T, g.CS, g.NJ
        t = g.tile
        s = sched[pi]
        first_tile = P and t == 0
        last_tile = P and t == 3
        hTk = lambda k: g.hT[k][:, :]
        hTkj = lambda k, j: g.hT[k][:, j * CS:(j + 1) * CS]

        if (not P) or t == 0:
            front(g, t)

        if kph <= 0:
            raise _Stop()
        for b in range(2):
            slot = use_block(s["gc"][b])
            proj_fm(g, slot, 8, hTk, 4, lambda m, bv: B.act(g.sgc[b * 4 + m][:, :], bv, AF.Sigmoid))
        if P:
            for c in range(8):
                B.cp("pool", g.glu[c][:, 0:30], halo[:, c, 0:30])
        for b in range(2):
            slot = use_block(s["au"][2 * b])
            proj_fm(g, slot, 8, hTk, 4, lambda m, bv: B.act(g.siga[m][:, :], bv, AF.Sigmoid))
            slot = use_block(s["au"][2 * b + 1])

            def ev_u(m, bv):
                c = b * 4 + m
                if P:
                    B.tt("dve", g.glu[c][:, 30:30 + T], bv, g.siga[m][:, :], ALU.mult)
                    if last_tile:
                        B.tt("dve", g.gtail[:, c, 0:30], bv.m(lambda a: a[:, 482:512]),
                             g.siga[m][:, 482:512], ALU.mult)
                else:
                    B.tt("dve", g.glu[c][:, :], bv, g.siga[m][:, :], ALU.mult)
            proj_fm(g, slot, 8, hTk, 4, ev_u)

        bsum = fw.ps(hold=True)
        bsq = fw.ps(hold=True)
        if P:
            for c in range(8):
                dg = g.diag[c % 2]
                B.tt("pool", dg[:, :, :], identb[:, :].m(lambda a: a.unsqueeze(1).broadcast_to([128, 31, 128])),
                     cwT.m(lambda a: a[:, c, :].unsqueeze(2).broadcast_to([128, 31, 128])), ALU.mult)
                bank = fw.ps()
                for w in range(31):
                    B.mm(bank[:, :], dg[:, w, :], g.glu[c][:, w:w + 512], w == 0, w == 30)
                B.act(g.cvT[c][:, :], bank[:, :], AF.Identity, bias=convb.m(lambda a: a[:, c:c + 1]))
                sqb = g.sq[c % 2]
                B.act(sqb[:, :], bank[:, :], AF.Square, bias=convb.m(lambda a: a[:, c:c + 1]))
                B.mm(bsum[:, 0:T], onesb[:, :], g.cvT[c][:, :], c == 0, c == 7)
                B.mm(bsq[:, 0:T], onesb[:, :], sqb[:, :], c == 0, c == 7)
            if not last_tile:
                for c in range(8):
                    B.cp("pool", halo[:, c, 0:30], g.glu[c][:, 512:542])
            else:
                for half in range(2):
                    bank = fw.ps()
                    for q in range(4):
                        c = half * 4 + q
                        B.tr(bank[0:30, q * 128:(q + 1) * 128], g.gtail[:, c, 0:30], identf[:, :])
                    B.cp("dve", g.ctail[0:30, half * 512:(half + 1) * 512], bank[0:30, :])
                B.dma(sp, DD["csp"], g.ctail[0:30, :])
        else:
            for c in range(8):
                B.stt("dve", g.cvs[:, c, :], g.glu[c][:, :], cwT.m(lambda a: a[:, c, 30:31]), cvs_p[:, c, :],
                      ALU.mult, ALU.add)
                B.act(g.cvT[c][:, :], g.cvs[:, c, :], AF.Identity, bias=convb.m(lambda a: a[:, c:c + 1]))
                sqb = g.sq[c % 2]
                B.act(sqb[:, :], g.cvs[:, c, :], AF.Square, bias=convb.m(lambda a: a[:, c:c + 1]))
                B.mm(bsum[:, 0:T], onesb[:, :], g.cvT[c][:, :], c == 0, c == 7)
                B.mm(bsq[:, 0:T], onesb[:, :], sqb[:, :], c == 0, c == 7)
            for half in range(2):
                bank = fw.ps()
                for q in range(4):
                    c = half * 4 + q
                    B.tr(bank[0:16, q * 128:(q + 1) * 128], g.glu[c][:, :], identf[:, :])
                B.cp("dve", g.ctail[0:16, half * 512:(half + 1) * 512], bank[0:16, :])
            B.dma(sp, DD["css"][:, 29, :], g.ctail[0:16, :])
        B.ts("dve", g.mean[:, :], bsum[:, 0:T], 1.0 / D, None, ALU.mult)
        B.tt("dve", g.msq[:, :], g.mean[:, :], g.mean[:, :], ALU.mult)
        B.stt("dve", g.rstd[:, :], bsq[:, 0:T], 1.0 / D, g.msq[:, :], ALU.mult, ALU.subtract)
        fw.release(bsum)
        fw.release(bsq)
        B.rsqrt(g.rstd[:, :], g.rstd[:, :], 1.0, 128)
        def ln_chunk(c):
            lt = g.lntmp if c % 2 == 0 else g.lntmpb
            B.tt("dve", lt[:, :], g.cvT[c][:, :], g.mean[:, :], ALU.subtract)
            B.tt("dve", lt[:, :], lt[:, :], g.rstd[:, :], ALU.mult)
            B.act(g.lnT[c][:, :], lt[:, :], AF.Silu, scale=lng.m(lambda a: a[:, c:c + 1]),
                  bias=lnb.m(lambda a: a[:, c:c + 1]))
        ln_next = [0]

        def ln_some(n):
            for _ in range(n):
                if ln_next[0] < 8:
                    ln_chunk(ln_next[0])
                    ln_next[0] += 1

        if kph <= 1:
            raise _Stop()
        tbuf = [g.tabq, g.tabk]

        def tab_load(i):
            if not P:
                return
            if i < 4:
                B.dma(sp, tbuf[i % 2][:, :, :], DD["tabq"][i, :, :, t * 512:(t + 1) * 512])
            else:
                B.dma(sp, tbuf[0][:, :, :], DD["tabk"][:, :, t * 512:(t + 1) * 512])
        if P:
            tab_load(0)
            tab_load(1)
            kcos, ksin = tbuf[0][:, 0, :], tbuf[0][:, 1, :]
        else:
            kcos, ksin = tabs[:, 0, :], tabs[:, 1, :]
        for b in range(2):
            slot = use_block(s["q"][b])
            for pr in range(2):
                h = b * 2 + pr
                if P:
                    if h >= 1:
                        tab_load(h + 1)
                    qcos, qsin = tbuf[h % 2][:, 0, :], tbuf[h % 2][:, 1, :]
                else:
                    qcos, qsin = tabs[:, 2 + 2 * h, :], tabs[:, 3 + 2 * h, :]
                ln_some(1)
                bks = [fw.ps(), fw.ps()]
                for mm_ in range(2):
                    m = pr * 2 + mm_
                    for k in range(8):
                        B.mm(bks[mm_][:, 0:T], slot[:, k, m * 128:(m + 1) * 128], hTk(k), k == 0, k == 7)
                rotary(g, bks[0][:, 0:T], bks[1][:, 0:T], qcos, qsin, g.qk[h * 2], g.qk[h * 2 + 1])
        for b in range(2):
            slot = use_block(s["k"][b])
            for pr in range(2):
                h = b * 2 + pr
                ln_some(1)
                bks = [fw.ps(), fw.ps()]
                for mm_ in range(2):
                    m = pr * 2 + mm_
                    for k in range(8):
                        B.mm(bks[mm_][:, 0:T], slot[:, k, m * 128:(m + 1) * 128], hTk(k), k == 0, k == 7)
                rotary(g, bks[0][:, 0:T], bks[1][:, 0:T], kcos, ksin, g.qk[8 + h * 2], g.qk[8 + h * 2 + 1])

        ln_some(8)
        for b in range(2):
            slot = use_block(s["wc"][b])
            proj_fm(g, slot, 8, lambda k: g.lnT[k][:, :], 4,
                    lambda m, bv: B.tt("dve", g.cpart[b * 4 + m][:, :], bv, g.sgc[b * 4 + m][:, :], ALU.mult))
        pend = []
        LAG = 4
        rcount = [0]

        def flush(n):
            while len(pend) > n:
                pend.pop(0)()

        def head_proj(h, hh):
            kT = [g.qk[8 + h * 2], g.qk[8 + h * 2 + 1]]
            slot = use_block(s["vg"][2 * h])
            proj_tm(g, slot, 8, hTkj, 512, lambda j, bv: B.act(g.vv[hh][j][:, :], bv, AF.Copy))
            slot = use_block(s["vg"][2 * h + 1])
            proj_tm(g, slot, 8, hTkj, 512, lambda j, bv: B.act(g.sgg[hh][j][:, :], bv, AF.Silu))
            bank = fw.ps()
            bb = bf(bank[:, :])
            for j in range(NJ):
                for half in range(2):
                    o0 = j * 256 + half * 128
                    B.tr(bb.m(lambda a: a[0:CS, o0:o0 + 128]), kT[half][:, j * CS:(j + 1) * CS], identb[:, :])
            kflat = g.khh[hh][:, :, :].m(lambda a: a.rearrange("p j d -> p (j d)"))
            if P:
                B.ts("dve", kflat, bb.m(lambda a: a[:, 0:1024]), kdec[:, h:h + 1], None, ALU.mult)
            else:
                B.cp("dve", kflat, bb.m(lambda a: a[0:16, 0:256]))

        pendB = []
        gnr = []
        for k in range(4):
            gnr.append(dict(st6=g.st6s[k], mv=g.mvs[k], gsd=g.gsds[k], gnb=g.gnbs[k]))

        def flushB(n):
            while len(pendB) > n:
                pendB.pop(0)()

        ccount = [0]

        def chunk(h, hh, j):
            qT = [g.qk[h * 2], g.qk[h * 2 + 1]]
            kT = [g.qk[8 + h * 2], g.qk[8 + h * 2 + 1]]
            v, sg, khat = g.vv[hh], g.sgg[hh], g.khh[hh]
            cs = slice(j * 128, (j + 1) * 128)
            n = rcount[0]
            rcount[0] += 1
            bk = fw.banks
            sb_ = bk[0]
            yb = bk[1 + n % 3]
            for half in range(2):
                B.mm(sb_[:, 0:128], kT[half][:, cs], qT[half][:, cs], half == 0, half == 1)
            Pm = g.Pm[n % 2]
            B.tt("dve", Pm[:, :], sb_[:, 0:128], dm[:, h * 128:(h + 1) * 128], ALU.mult)
            for half in range(2):
                sbk = bk[4 + half]
                B.mm(sbk[:, :], khat[:, j, half * 128:(half + 1) * 128], v[j][:, :], True, True)
            B.mm(yb[:, :], Pm[:, :], v[j][:, :], True, False)
            for half in range(2):
                B.mm(yb[:, :], qT[half][:, cs], S16[h * 2 + half][:, :], False, half == 1)
            for half in range(2):
                sbk = bk[4 + half]
                B.stt("dve", S32[h * 2 + half][:, :], S32[h * 2 + half][:, :], float(gam128[h]),
                      sbk[:, :], ALU.mult, ALU.add)
                if half == 0:
                    B.act(S16[h * 2 + half][:, :], S32[h * 2 + half][:, :], AF.Copy)
                else:
                    B.cp("pool", S16[h * 2 + half][:, :], S32[h * 2 + half][:, :])
            rn = g.rn[n % 2]
            r = g.r[n % len(g.r)]
            gb = gnr[n % 4]
            ybv = yb[:, :]
            B.fw.emit("dve", lambda e: e.bn_stats(out=gb["st6"][:, :].ap, in_=ybv.ap),
                      reads=[ybv.tt], writes=[gb["st6"]])
            B.fw.emit("dve", lambda e: e.bn_aggr(out=gb["mv"][:, :].ap, in_=gb["st6"][:, :].ap),
                      reads=[gb["st6"]], writes=[gb["mv"]])
            B.act(gb["gsd"][:, :], gb["mv"][:, 1:2], AF.Sqrt, bias=B.epsc[:, 0:1], scale=1.0)

            def B2():
                B.recip(gb["gsd"][:, :], gb["gsd"][:, :])
                B.stt("dve", gb["gnb"][:, :], gb["mv"][:, 0:1], -1.0, gb["gsd"][:, :], ALU.mult, ALU.mult)
                B.act(rn[:, :], ybv, AF.Identity, scale=gb["gsd"][:, 0:1], bias=gb["gnb"][:, 0:1])
                B.tt("pool", r[:, :], rn[:, :], sg[j][:, :], ALU.mult)

                def C():
                    tb = bk[6 + ccount[0] % 2]
                    ccount[0] += 1
                    tbb = bf(tb[:, :])
                    for q4 in range(4):
                        B.tr(tbb.m(lambda a: a[:, q4 * 128:(q4 + 1) * 128]),
                             r[:, q4 * 128:(q4 + 1) * 128], identb[:, :])
                    B.cp("dve", g.rT[h][:, :, cs],
                         tbb.m(lambda a: a[:, 0:512].rearrange("p (q t) -> p q t", q=4)))
                pend.append(C)
            pendB.append(B2)
            flushB(1)
            flush(LAG)

        if P:
            for hp in range(0, NH, 2):
                flushB(0)
                for hh in range(2):
                    head_proj(hp + hh, hh)
                for j in range(NJ):
                    for hh in range(2):
                        chunk(hp + hh, hh, j)
                if last_tile:
                    for hh in range(2):
                        h = hp + hh
                        for half in range(2):
                            B.dma(sp, DD["rsp"][h, half * 128:(half + 1) * 128, :], S32[h * 2 + half][:, :])
            bkx = fw.banks
            grb = [[bkx[0], bkx[4], bkx[5], bkx[6]], [bkx[7], bkx[0], bkx[4], bkx[5]]]
            for b in range(2):
                slot = use_block(s["gr"][b])
                proj_fm(g, slot, 8, hTk, 4, lambda m, bv: B.act(g.sgr[b * 4 + m][:, :], bv, AF.Sigmoid),
                        banks=grb[b])
            flushB(0)
            flush(0)
        else:
            for h in range(NH):
                qT = [g.qk[h * 2], g.qk[h * 2 + 1]]
                head_proj(h, 0)
                for c in range(2):
                    B.tt("dve", g.qTm[:, c, :, :],
                         qT[c][:, :].m(lambda a: a.unsqueeze(1).broadcast_to([128, 16, 16])),
                         diag16[:, :].m(lambda a: a.rearrange("p (s t) -> p s t", t=16)), ALU.mult)
                ob = fw.ps(hold=True)
                pend_o = []

                def flush_o(n):
                    while len(pend_o) > n:
                        pend_o.pop(0)()
                for smp in range(16):
                    gi = h * 16 + smp
                    st_load_upto(gi + NST - 1)
                    st = g.st[gi % NST]
                    km = g.km[smp % 2]
                    B.ts("dve", km[:, :], g.khh[0][:, 0, :], identf[0:16, smp:smp + 1], None, ALU.mult)
                    for c in range(2):
                        kb = fw.ps()
                        B.mm(kb[:, :], km[:, c * 128:(c + 1) * 128], g.vv[0][0][:, :], True, True)
                        B.stt("dve", st[:, c, :], st[:, c, :], float(gam[h]), kb[:, :], ALU.mult, ALU.add)
                    stb = g.stb[gi % 4]
                    B.act(stb[:, :, :], st[:, :, :], AF.Copy)
                    B.dma(sp, DD["rss"][smp, h].rearrange("(c p) v -> p c v", p=128), st[:, :, :])

                    def mk_o(smp=smp, stb=stb):
                        def o_():
                            for c in range(2):
                                B.mm(ob[0:16, :], g.qTm[:, c, smp, :], stb[:, c, :], smp == 0 and c == 0,
                                     smp == 15 and c == 1)
                        return o_
                    pend_o.append(mk_o())
                    flush_o(2)
                flush_o(0)
                rn, r = g.rn[h % 2], g.r[h % 2]
                gn_gate(g, ob[0:16, :], g.sgg[0][0][:, :], rn, r)
                fw.release(ob)
                tb = fw.ps()
                tbb = bf(tb[:, :])
                for q4 in range(4):
                    B.tr(tbb.m(lambda a: a[:, q4 * 16:(q4 + 1) * 16]), r[:, q4 * 128:(q4 + 1) * 128],
                         identb[0:16, 0:16])
                B.act(g.rT[h][:, :, :], tbb.m(lambda a: a[:, 0:64].rearrange("p (q t) -> p q t", q=4)), AF.Copy)

        if kph <= 2:
            raise _Stop()
        if P:
            for j in range(4):
                B.dma(sp, g.xs1[j][:, :], DD["xp"][t * 512 + j * 128:t * 512 + (j + 1) * 128, :])
        if not P:
            for b in range(2):
                slot = use_block(s["gr"][b])
                proj_fm(g, slot, 8, hTk, 4, lambda m, bv: B.act(g.sgr[b * 4 + m][:, :], bv, AF.Sigmoid))
        if kph <= 2.2:
            raise _Stop()
        for ch in range(2):
            banks = [fw.ps(hold=True) for _ in range(4)]
            for rh in range(2):
                slot = use_block(s["wr"][ch * 2 + rh], cache=False)
                if pi == 0:
                    for k8 in range(8):
                        B.act(slot[:, k8, :], slot[:, k8, :], AF.Copy,
                              scale=gng.m(lambda a: a[:, rh * 8 + k8:rh * 8 + k8 + 1]))
                    cache_block(s["wr"][ch * 2 + rh])

                def ev_r(m, bv):
                    c = ch * 4 + m
                    B.tt("dve", g.mtmp[:, :], bv, g.sgr[c][:, :], ALU.mult)
                    B.tt("dve", g.merged[c][:, :], g.mtmp[:, :], g.cpart[c][:, :], ALU.add)
                proj_fm(g, slot, 8, lambda k: g.rT[(rh * 8 + k) // 4][:, (rh * 8 + k) % 4, :], 4, ev_r,
                        banks=banks, first=rh == 0, last=rh == 1)
            for bk in banks:
                fw.release(bk)
        if kph <= 2.5:
            raise _Stop()
        wslots = [use_block(s["wo"][0]), use_block(s["wo"][1], keep=1)]
        for j in range(NJ):
            for ch in range(2):
                bank = fw.ps()
                for k in range(8):
                    B.mm(bank[0:CS, 0:512], g.merged[k][:, j * CS:(j + 1) * CS], wslots[ch][:, k, 0:512],
                         k == 0, k == 7)
                sq_evac(g, g.mix[j][:, ch * 512:(ch + 1) * 512], bank[0:CS, 0:512], None)
        if kph <= 2.8:
            raise _Stop()
        resid(g, g.mix, g.ssm, G1bc, None if P else g.G1s)

        if kph <= 3:
            raise _Stop()
        norm_to_hT(g, A2T, Bv2T, g.xs1)
        for i in range(6):
            nm = 4 if i < 5 else 2
            slot = use_block(s["gu"][2 * i])
            (proj_fm_kouter if i == 0 else proj_fm)(g, slot, 8, hTk, nm,
                                                    lambda m, bv: B.act(g.sgt[m][:, :], bv, AF.Silu))
            slot = use_block(s["gu"][2 * i + 1])
            proj_fm(g, slot, 8, hTk, nm,
                    lambda m, bv: B.tt("dve", g.aT[i * 4 + m][:, :], bv, g.sgt[m][:, :], ALU.mult))
        if P and t < 3:
            front(g, t + 1)
        for ch in range(2):
            banks = [fw.ps(hold=True) for _ in range(NJ)]
            for rb in range(3):
                slot = use_block(s["wd"][ch * 3 + rb])
                nk = 8 if rb < 2 else 6
                proj_tm(g, slot, nk, lambda k, j: g.aT[rb * 8 + k][:, j * CS:(j + 1) * CS], 512,
                        lambda j, bv: sq_evac(g, g.ffo[j][:, ch * 512:(ch + 1) * 512], bv,
                                              g.ssm[0:CS, j * 2 + ch:j * 2 + ch + 1]),
                        banks=banks, first=rb == 0, last=rb == 2)
            for bk in banks:
                fw.release(bk)
        if P:
            resid(g, g.ffo, g.ssm, G2bc, None,
                  after=lambda j: B.dma(sp, DD["yp"][t * 512 + j * 128:t * 512 + (j + 1) * 128, :],
                                        g.xs1[j][:, :]))
        else:
            resid(g, g.ffo, g.ssm, G2bc, g.G2s, after=lambda j: B.dma(sp, DD["ys"], g.xs1[0][:, :]))

    import os
    kstop = int(os.environ.get("KSTOP", "99"))
    for pi, (kind, t) in enumerate(passes):
        if pi >= kstop:
            break
        if kind == "s":
            run_pass(pi, gs)
        else:
            gp.tile = t
            run_pass(pi, gp)

    fw.final_wait("sp")
    with ExitStack() as st:
        fw.generate(st)
    return B.nc


_CACHE = {}


def _consts():
    if "c" in _CACHE:
        return _CACHE["c"]
    lg = np.log(1.0 - np.exp(np.linspace(np.log(1.0 / 32.0), np.log(1.0 / 512.0), NH)))
    inv_freq = (np.float32(10000.0) ** (-np.arange(0, 256, 2, dtype=np.float32) / np.float32(256.0)))
    inv_freq = inv_freq.astype(np.float32)
    pos = np.arange(2048, dtype=np.float32)
    ang = (inv_freq[:, None] * pos[None, :]).astype(np.float32).astype(np.float64)
    cos, sin = np.cos(ang), np.sin(ang)
    tabk = np.stack([cos, sin], axis=1).astype(np.float32)
    i_in = (np.arange(2048) % 128).astype(np.float64)
    tabq = np.zeros((4, 128, 2, 2048), np.float32)
    for h in range(NH):
        dec = np.exp(lg[h] * (i_in + 1.0)) / 16.0
        tabq[h, :, 0, :] = cos * dec[None, :]
        tabq[h, :, 1, :] = sin * dec[None, :]
    angs = (inv_freq * np.float32(16384.0)).astype(np.float32).astype(np.float64)
    tabs = np.zeros((128, 10, 16), np.float32)
    tabs[:, 0, :] = np.cos(angs)[:, None]
    tabs[:, 1, :] = np.sin(angs)[:, None]
    for h in range(NH):
        tabs[:, 2 + 2 * h, :] = (np.cos(angs) / 16.0)[:, None]
        tabs[:, 3 + 2 * h, :] = (np.sin(angs) / 16.0)[:, None]
    j = np.arange(128, dtype=np.float64)
    dm = np.zeros((128, 4, 128), np.float32)
    kdec = np.zeros((128, 4), np.float32)
    mask = (j[None, :] >= j[:, None])
    for h in range(NH):
        dm[:, h, :] = np.exp(-lg[h] * (j + 1.0))[:, None] * mask
        kdec[:, h] = np.exp(lg[h] * (127.0 - j))
    sel = np.zeros((17, 128), np.float32)
    sel[16, :] = 1.0
    diag16 = np.broadcast_to(np.eye(16, dtype=np.float32).reshape(1, 256), (128, 256)).copy()
    c = dict(tabk=tabk, tabq=tabq, tabs=tabs, dm=dm.reshape(128, 512), kdec=kdec,
             ident=np.eye(128, dtype=np.float32), sel=sel, diag16=diag16)
    _CACHE["c"] = c
    return c


def _fm(v):
    return np.ascontiguousarray(v.reshape(-1, 128).T)


def kernel(x_prompt, x_sample, c_prompt, c_sample, state_ret, state_conv, w_in, w_ada, b_ada,
           g_pre1, g_post1, g_pre2, g_post2, conv_w, conv_b, conv_ln_g, conv_ln_b, w_conv_out,
           ret_gn_g, w_ret_out, w_out, w_ffn_gate, w_ffn_up, w_ffn_down):
    f = lambda a: np.ascontiguousarray(np.asarray(a, dtype=np.float32))
    x_prompt, x_sample, c_prompt, c_sample = f(x_prompt), f(x_sample), f(c_prompt), f(c_sample)
    state_ret, state_conv = f(state_ret), f(state_conv)
    cst = _consts()
    if "nc" not in _CACHE:
        _CACHE["nc"] = build_program()
    nc = _CACHE["nc"]
    cw = f(conv_w)[0]
    cwT = np.ascontiguousarray(cw.T.reshape(8, 128, 31).transpose(1, 0, 2)).reshape(128, 248)
    pp = np.concatenate([_fm(f(g_pre1)[0]), _fm(f(g_pre2)[0]), _fm(f(conv_b)[0]), _fm(f(conv_ln_g)[0]),
                         _fm(f(conv_ln_b)[0]), _fm(f(ret_gn_g)[0]), cwT], axis=1).astype(np.float32)
    gpost = np.stack([f(g_post1)[0], f(g_post2)[0]], axis=0)
    shared = dict(w_in=f(w_in)[0], w_ada=f(w_ada)[0], b_ada=f(b_ada), w_conv_out=f(w_conv_out)[0],
                  w_ret_out=f(w_ret_out)[0], w_out=f(w_out)[0], w_gate=f(w_ffn_gate)[0],
                  w_up=f(w_ffn_up)[0], w_down=f(w_ffn_down)[0], pp=np.ascontiguousarray(pp), gpost=gpost,
                  **cst)
    in_maps = []
    for i in range(8):
        call = np.concatenate([c_sample[16 * i:16 * i + 16], c_prompt[i:i + 1]], axis=0)
        cT = np.ascontiguousarray(call.T.reshape(8, 128, 17).transpose(1, 0, 2)).reshape(128, 136)
        m = dict(shared)
        m.update(xp=x_prompt[i], xs=np.ascontiguousarray(x_sample[16 * i:16 * i + 16, 0, :]), cT=cT,
                 sret=state_ret[0, 16 * i:16 * i + 16], sconv=state_conv[0, 16 * i:16 * i + 16])
        in_maps.append(m)
    import os
    ncores = int(os.environ.get("NCORES", "8"))
    res = run_bass_kernel_spmd(nc, in_maps[:ncores], core_ids=list(range(ncores)))
    R = list(res.results) + [res.results[0]] * (8 - ncores)
    y_prompt = np.stack([R[i]["yp"] for i in range(8)], axis=0)
    y_sample = np.concatenate([R[i]["ys"] for i in range(8)], axis=0)[:, None, :]
    rsp = np.stack([R[i]["rsp"] for i in range(8)], axis=0)[None]
    rss = np.concatenate([R[i]["rss"] for i in range(8)], axis=0)[None]
    csp = np.stack([R[i]["csp"] for i in range(8)], axis=0)[None]
    css = np.concatenate([R[i]["css"] for i in range(8)], axis=0)[None]
    return (y_prompt.astype(np.float32), y_sample.astype(np.float32), rsp.astype(np.float32),
            rss.astype(np.float32), csp.astype(np.float32), css.astype(np.float32))
```

```python
from contextlib import ExitStack
import numpy as np
import concourse.bass as bass
import concourse.mybir as mybir
from concourse.bass_utils import run_bass_kernel_spmd

F32 = mybir.dt.float32
BF16 = mybir.dt.bfloat16
AF = mybir.ActivationFunctionType
ALU = mybir.AluOpType
_DT_SIZE = {F32: 4, BF16: 2}

D = 1024
NH = 4
EPS = 1e-6
NS = 7
NPP = 8 * 5 + 16 + 8 * 31


class View:
    def __init__(self, tt, ap):
        self.tt = tt
        self.ap = ap

    def m(self, f):
        return View(self.tt, f(self.ap))

    def __getitem__(self, k):
        return View(self.tt, self.ap[k])


class TT:
    def __init__(self, name, space, lo, hi, handle):
        self.name = name
        self.space = space
        self.lo = lo
        self.hi = hi
        self.h = handle
        self.dsem = None
        self.dcount = 0

    def __getitem__(self, k):
        return View(self, self.h[k])


class Op:
    __slots__ = ("eng", "fn", "waits", "signal", "sigval", "is_dma", "dtok", "idx", "lbl")


class FW:
    ALLENG = ("pe", "act", "dve", "pool", "sp")

    def __init__(self, nc):
        self.nc = nc
        self.ops = {e: [] for e in self.ALLENG}
        self.recs = []
        self.waited = {e: {} for e in self.ALLENG}
        self.dma_sems = []
        self.uid = 0
        self.ARENA = 204800
        arena = nc.alloc_sbuf_tensor("arena", [128, self.ARENA // 4], F32)
        self.base = nc.lookup_mloc(arena).addr
        self.banks = []
        for i in range(8):
            h = nc.alloc_psum_tensor(f"psb{i}", [128, 512], F32)
            self.banks.append(TT(f"psb{i}", "ps", i * 2048, (i + 1) * 2048, h))
        self.held = set()
        self.rr = 0

    def sb(self, name, shape, dtype, off):
        nbytes = int(np.prod(shape[1:])) * _DT_SIZE[dtype]
        assert off % 4 == 0 and off + nbytes <= self.ARENA, (name, off, nbytes)
        self.uid += 1
        h = self.nc.alloc_sbuf_tensor_at(f"{name}_{self.uid}", list(shape), dtype,
                                         offset=self.base + off)
        return TT(name, "sb", off, off + nbytes, h)

    def ps(self, hold=False):
        for _ in range(8):
            b = self.banks[self.rr]
            self.rr = (self.rr + 1) % 8
            if b not in self.held:
                if hold:
                    self.held.add(b)
                return b
        raise RuntimeError("no psum bank")

    def release(self, b):
        self.held.discard(b)

    def _deps(self, reads, writes):
        deps = []
        for t in reads:
            for r in self.recs:
                if r[3] == "w" and r[0] == t.space and r[1] < t.hi and t.lo < r[2]:
                    deps.append(r[4])
        for t in writes:
            for r in self.recs:
                if r[0] == t.space and r[1] < t.hi and t.lo < r[2]:
                    deps.append(r[4])
        return deps

    def _record(self, tok, reads, writes):
        for t in writes:
            self.recs = [r for r in self.recs
                         if not (r[0] == t.space and t.lo <= r[1] and r[2] <= t.hi)]
            self.recs.append([t.space, t.lo, t.hi, "w", tok])
        for t in reads:
            if tok[0] == "c":
                found = False
                for r in self.recs:
                    if (r[3] == "r" and r[0] == t.space and r[1] == t.lo and r[2] == t.hi
                            and r[4][0] == "c" and r[4][1].eng == tok[1].eng):
                        r[4] = tok
                        found = True
                        break
                if found:
                    continue
            self.recs.append([t.space, t.lo, t.hi, "r", tok])

    def _mk_waits(self, eng, deps):
        best = {}
        for d in deps:
            if d[0] == "c":
                o = d[1]
                if o.eng == eng and eng == "pe":
                    continue
                key = ("c", o.eng)
                if key not in best or best[key][1].idx < o.idx:
                    best[key] = d
            else:
                key = ("d", id(d[1]))
                if key not in best or best[key][2] < d[2]:
                    best[key] = d
        waits = []
        w = self.waited[eng]
        for key, d in best.items():
            v = d[1].idx if d[0] == "c" else d[2]
            if w.get(key, -1) >= v:
                continue
            w[key] = v
            if d[0] == "c":
                d[1].signal = True
            waits.append(d)
        return waits

    def _newop(self, eng, fn):
        op = Op()
        op.eng = eng
        op.fn = fn
        op.signal = False
        op.sigval = None
        op.is_dma = False
        op.dtok = None
        op.idx = len(self.ops[eng])
        op.lbl = getattr(self, "lbl", "")
        return op

    def emit(self, eng, fn, reads=(), writes=()):
        op = self._newop(eng, fn)
        op.waits = self._mk_waits(eng, self._deps(reads, writes))
        self.ops[eng].append(op)
        self._record(("c", op), reads, writes)
        return op

    def dma(self, queue, fn, semtt, reads=(), writes=()):
        op = self._newop(queue, fn)
        op.is_dma = True
        if semtt.dsem is None:
            semtt.dsem = True
            self.dma_sems.append(semtt)
        semtt.dcount += 16
        tok = ("d", semtt, semtt.dcount)
        op.dtok = tok
        op.waits = self._mk_waits(queue, self._deps(reads, writes))
        self.ops[queue].append(op)
        self._record(tok, reads, writes)
        return op

    def final_wait(self, queue="sp"):
        op = self._newop(queue, None)
        op.waits = self._mk_waits(queue, [r[4] for r in self.recs])
        self.ops[queue].append(op)

    def generate(self, stack):
        nc = self.nc
        esem = {e: stack.enter_context(nc.semaphore(f"s_{e}")) for e in self.ALLENG}
        for i, t in enumerate(self.dma_sems):
            t.dsem = stack.enter_context(nc.semaphore(f"d_{i}"))
        for e in self.ALLENG:
            c = 0
            for op in self.ops[e]:
                if op.signal:
                    c += 1
                    op.sigval = c
        block = stack.enter_context(nc.Block())
        engmap = {"pe": block.tensor, "act": block.scalar, "dve": block.vector,
                  "pool": block.gpsimd, "sp": block.sync}

        def run(e):
            def body(engine):
                for op in self.ops[e]:
                    for d in op.waits:
                        if d[0] == "c":
                            engine.wait_ge(esem[d[1].eng], d[1].sigval)
                        else:
                            engine.wait_ge(d[1].dsem, d[2])
                    if op.fn is None:
                        continue
                    ins = op.fn(engine)
                    if op.is_dma:
                        ins.then_inc(op.dtok[1].dsem, 16)
                    elif op.signal:
                        ins.then_inc(esem[e], 1)
            return body

        for e in self.ALLENG:
            engmap[e](run(e))


def _tts(*vs):
    return [v.tt for v in vs if isinstance(v, View)]


def _a(v):
    return v.ap if isinstance(v, View) else v


class Builder:
    def __init__(self):
        self.nc = bass.Bass("TRN2", target_bir_lowering=False)
        self.fw = FW(self.nc)
        self.D = {}

    def din(self, name, shape):
        self.D[name] = self.nc.dram_tensor(name, list(shape), F32, kind="ExternalInput").ap()

    def dout(self, name, shape):
        self.D[name] = self.nc.dram_tensor(name, list(shape), F32, kind="ExternalOutput").ap()

    def mm(self, out, lhsT, rhs, start, stop):
        self.fw.emit("pe", lambda e: e.matmul(out.ap, lhsT=lhsT.ap, rhs=rhs.ap, start=start, stop=stop),
                     reads=_tts(lhsT, rhs), writes=[out.tt])

    def tr(self, out, in_, ident):
        self.fw.emit("pe", lambda e: e.transpose(out.ap, in_.ap, ident.ap),
                     reads=_tts(in_, ident), writes=[out.tt])

    def act(self, out, in_, func, bias=None, scale=None, accum=None):
        kw = {}
        if bias is not None:
            kw["bias"] = _a(bias)
        if scale is not None:
            kw["scale"] = _a(scale)
        if accum is not None:
            kw["accum_out"] = accum.ap
        self.fw.emit("act", lambda e: e.activation(out=out.ap, in_=in_.ap, func=func, **kw),
                     reads=_tts(in_, bias, scale), writes=_tts(out, accum))

    def tt(self, eng, out, in0, in1, op):
        self.fw.emit(eng, lambda e: e.tensor_tensor(out=out.ap, in0=in0.ap, in1=in1.ap, op=op),
                     reads=_tts(in0, in1), writes=[out.tt])

    def ts(self, eng, out, in0, s1, s2, op0, op1=None):
        if op1 is None:
            f = lambda e: e.tensor_scalar(out=out.ap, in0=in0.ap, scalar1=_a(s1), scalar2=None, op0=op0)
        else:
            f = lambda e: e.tensor_scalar(out=out.ap, in0=in0.ap, scalar1=_a(s1), scalar2=_a(s2),
                                          op0=op0, op1=op1)
        self.fw.emit(eng, f, reads=_tts(in0, s1, s2), writes=[out.tt])

    def stt(self, eng, out, in0, scalar, in1, op0, op1):
        self.fw.emit(eng, lambda e: e.scalar_tensor_tensor(out=out.ap, in0=in0.ap, scalar=_a(scalar),
                                                           in1=in1.ap, op0=op0, op1=op1),
                     reads=_tts(in0, scalar, in1), writes=[out.tt])

    def cp(self, eng, out, in_):
        self.fw.emit(eng, lambda e: e.tensor_copy(out=out.ap, in_=in_.ap), reads=[in_.tt], writes=[out.tt])

    def memset(self, eng, out, val):
        self.fw.emit(eng, lambda e: e.memset(out.ap, val), writes=[out.tt])

    def recip(self, out, in_):
        self.fw.emit("dve", lambda e: e.reciprocal(out=out.ap, in_=in_.ap), reads=[in_.tt], writes=[out.tt])

    def dma(self, queue, out, in_, xr=(), xw=()):
        xr, xw = list(xr), list(xw)
        if isinstance(out, View) and isinstance(in_, View):
            raise NotImplementedError
        if isinstance(out, View):
            self.fw.dma(queue, lambda e: e.dma_start(out=out.ap, in_=in_), out.tt, reads=xr, writes=[out.tt] + xw)
        elif isinstance(in_, View):
            self.fw.dma(queue, lambda e: e.dma_start(out=out, in_=in_.ap), in_.tt, reads=[in_.tt] + xr, writes=xw)
        else:
            self.fw.dma(queue, lambda e: e.dma_start(out=out, in_=in_), self.dummy, reads=xr,
                        writes=[self.dummy] + xw)

    def rsqrt(self, out, in_, scale, CS):
        self.act(out, in_, AF.Sqrt, bias=self.epsc[0:CS, 0:1], scale=scale)
        self.recip(out, out)


class G:
    pass


def bf(v):
    return v.m(lambda a: a.bitcast(BF16))


def build_program():
    B = Builder()
    fw = B.fw
    DD = B.D
    for n, s in [("xp", [2048, D]), ("xs", [16, D]), ("cT", [128, 8 * 17]), ("sret", [16, 4, 256, 512]),
                 ("sconv", [16, 30, D]), ("w_in", [D, 10240]), ("w_ada", [D, 6144]), ("b_ada", [1, 6144]),
                 ("w_conv_out", [D, D]), ("w_ret_out", [2048, D]), ("w_out", [D, D]),
                 ("w_gate", [D, 2816]), ("w_up", [D, 2816]), ("w_down", [2816, D]),
                 ("pp", [128, NPP]), ("gpost", [2, D]), ("tabk", [128, 2, 2048]),
                 ("tabq", [4, 128, 2, 2048]), ("tabs", [128, 10, 16]), ("dm", [128, 512]),
                 ("kdec", [128, 4]), ("ident", [128, 128]), ("sel", [17, 128]), ("diag16", [128, 256])]:
        B.din(n, s)
    for n, s in [("yp", [2048, D]), ("ys", [16, D]), ("rsp", [4, 256, 512]), ("rss", [16, 4, 256, 512]),
                 ("csp", [30, D]), ("css", [16, 30, D])]:
        B.dout(n, s)

    NBP = 46
    wscr = B.nc.dram_tensor("wscr", [NBP, 128, 4096], BF16, kind="Internal").ap()
    gscr = B.nc.dram_tensor("gscr", [2, 16, D], F32, kind="Internal").ap()
    wscr_tt = [TT(f"wscr{b}", "dram", b, b + 1, None) for b in range(NBP)]
    gscr_tt = [TT(f"gscr{b}", "dram", 1000 + b, 1001 + b, None) for b in range(2)]
    off = 0
    slots = []
    for i in range(NS):
        slots.append(fw.sb(f"slot{i}", [128, 8, 512], BF16, off))
        off += 8192
    S32 = []
    for i in range(8):
        S32.append(fw.sb(f"S32_{i}", [128, 512], F32, off))
        off += 2048
    S16 = []
    for i in range(8):
        S16.append(fw.sb(f"S16_{i}", [128, 512], BF16, off))
        off += 1024
    coff = [off]

    def calloc(name, shape, dt):
        n = int(np.prod(shape[1:])) * _DT_SIZE[dt]
        t = fw.sb(name, shape, dt, coff[0])
        coff[0] += (n + 31) // 32 * 32
        return t

    identb = calloc("identb", [128, 128], BF16)
    identf = calloc("identf", [128, 128], F32)
    onesb = calloc("onesb", [128, 128], BF16)
    dm = calloc("dm", [128, 512], F32)
    G1bc = calloc("G1bc", [128, D], F32)
    G2bc = calloc("G2bc", [128, D], F32)
    pp = calloc("pp", [128, NPP], F32)
    modT = calloc("modT", [128, 48, 17], F32)
    A1T = calloc("A1T", [128, 8, 17], F32)
    A2T = calloc("A2T", [128, 8, 17], F32)
    kdec = calloc("kdec", [128, 4], F32)
    diag16 = calloc("diag16", [128, 256], F32)
    sel = calloc("sel", [17, 128], F32)
    B.epsc = calloc("epsc", [128, 1], F32)
    B.dummy = calloc("dummy", [128, 1], F32)
    scT = calloc("scT", [128, 8 * 17], BF16)
    cvs_p = calloc("cvs_p", [128, 8, 16], F32)
    tabs = calloc("tabs", [128, 10, 16], F32)
    halo = calloc("halo", [128, 8, 32], BF16)
    assert coff[0] - off <= 20480, coff[0] - off
    off += 20480
    R0 = off
    RSZ = fw.ARENA - R0
    assert RSZ >= 102400, RSZ

    gpre1 = pp[:, 0:8]
    gpre2 = pp[:, 8:16]
    convb = pp[:, 16:24]
    lng = pp[:, 24:32]
    lnb = pp[:, 32:40]
    gng = pp[:, 40:56]
    cwT = pp[:, 56:56 + 248].m(lambda a: a.rearrange("p (c w) -> p c w", w=31))
    Bv1T = modT[:, 0:8, :]
    Bv2T = modT[:, 24:32, :]

    lg = np.log(1.0 - np.exp(np.linspace(np.log(1.0 / 32.0), np.log(1.0 / 512.0), NH)))
    gam = np.exp(lg)
    gam128 = np.exp(lg * 128.0)

    def mk_group(kind, tile):
        g = G()
        g.kind = kind
        g.tile = tile
        P = kind == "p"
        g.T = 512 if P else 16
        g.CS = 128 if P else 16
        g.NJ = 4 if P else 1
        T, CS, NJ = g.T, g.CS, g.NJ
        o = [R0]

        def al(name, shape, dt, at=None):
            if at is not None:
                o[0] = at
            n = int(np.prod(shape[1:])) * _DT_SIZE[dt]
            t = fw.sb(f"{kind}{name}", shape, dt, o[0])
            o[0] += (n + 31) // 32 * 32
            return t

        g.xs = None
        g.hT = [al(f"hT{k}", [128, T], BF16) for k in range(8)]
        g.cpart = [al(f"cp{k}", [128, T], BF16) for k in range(8)]
        PH = R0 + 16384 if P else o[0]
        g.sgc = [al(f"sgc{k}", [128, T], BF16, at=PH if k == 0 else None) for k in range(8)]
        g.siga = [al(f"siga{k}", [128, T], BF16) for k in range(4)]
        g.glu = [al(f"glu{k}", [128, 544 if P else 16], BF16 if P else F32) for k in range(8)]
        g.diag = [al(f"diag{k}", [128, 31, 128], BF16) for k in range(2)] if P else None
        g.cvT = [al(f"cvT{k}", [128, T], BF16) for k in range(8)]
        g.sq = [al(f"sq{k}", [128, T], BF16) for k in range(2)]
        g.mean = al("mean", [128, T], F32)
        g.rstd = al("rstd", [128, T], F32)
        g.msq = al("msq", [128, T], F32)
        g.lntmp = al("lntmp", [128, T], F32)
        g.lntmpb = al("lntmpb", [128, T], F32)
        g.lnT = [al(f"lnT{k}", [128, T], BF16) for k in range(8)]
        e_ln = o[0]
        g.gtail = al("gtail", [128, 8, 32], F32)
        g.ctail = al("ctail", [32, D], F32)
        if not P:
            g.strow = [al(f"strow{k}", [120, D], F32) for k in range(1)]
            g.cvs = al("cvs", [128, 8, 16], F32)
            g.cvtmp = al("cvtmp", [128, 120], F32)
        e1 = o[0]
        g.rT = [al(f"rT{k}", [128, 4, T], BF16, at=(PH if P else e_ln) if k == 0 else None) for k in range(4)]
        g.qk = [al(f"qk{k}", [128, T], BF16, at=(PH + 16384 if (P and k == 0) else None)) for k in range(16)]
        g.tmp = [al(f"tmp{k}", [128, T], F32, at=(PH + 68608 if (P and k == 0) else None)) for k in range(4)]
        g.tabk = al("tabk", [128, 2, T], F32)
        g.tabq = al("tabq", [128, 2, T], F32)
        HG = 2 if P else 1
        g.vv = [[al(f"v{hh}_{k}", [CS, 512], BF16, at=(PH + 36864 if (P and hh == 0 and k == 0) else None))
                 for k in range(NJ)] for hh in range(HG)]
        g.sgg = [[al(f"sg{hh}_{k}", [CS, 512], BF16) for k in range(NJ)] for hh in range(HG)]
        g.khh = [al(f"khat{hh}", [CS, NJ, 256], BF16) for hh in range(HG)]
        g.Pm = [al(f"Pm{k}", [128, 128], BF16) for k in range(2)]
        g.rn = [al(f"rn{k}", [CS, 512], BF16) for k in range(2)]
        g.r = [al(f"r{k}", [CS, 512], BF16) for k in range(6 if P else 2)]
        g.st6s = [al(f"st6_{k}", [128, 6], F32) for k in range(4)]
        g.mvs = [al(f"mv_{k}", [128, 2], F32) for k in range(4)]
        g.gsds = [al(f"gsd_{k}", [128, 1], F32) for k in range(4)]
        g.gnbs = [al(f"gnb_{k}", [128, 1], F32) for k in range(4)]
        g.st6 = al("st6", [128, 6], F32)
        g.mv = al("mv", [128, 2], F32)
        g.gsd = al("gsd", [128, 1], F32)
        g.gnb = al("gnb", [128, 1], F32)
        if not P:
            g.st = None
            g.km = [al(f"km{k}", [16, 256], BF16) for k in range(2)]
            g.qTm = al("qTm", [128, 2, 16, 16], BF16)
        e2 = max(o[0], PH + 84992) if P else o[0]
        g.sgr = [al(f"sgr{k}", [128, T], BF16, at=(PH + 16384 if P else e2) if k == 0 else None)
                 for k in range(8)]
        g.merged = [al(f"mg{k}", [128, T], BF16) for k in range(8)]
        g.mix = [al(f"mix{k}", [CS, D], F32) for k in range(NJ)]
        g.mtmp = al("mtmp", [128, T], F32)
        g.junk = al("junk", [CS, 512], BF16)
        g.ssm = al("ssm", [128, 8], F32)
        g.ss2 = al("ss2", [128, 4], F32)
        e3 = o[0]
        g.xn = [al(f"xn{k}", [CS, D], BF16, at=PH if k == 0 else None) for k in range(NJ)]
        g.aT = [al(f"aT{k}", [128, T], BF16) for k in range(22)]
        g.sgt = [al(f"sgt{k}", [128, T], BF16) for k in range(4)]
        g.ffo = [al(f"ffo{k}", [CS, D], F32) for k in range(NJ)]
        g.ssj = [al(f"ssj{k}", [128, 1], F32) for k in range(NJ)]
        g.rsj = [al(f"rsj{k}", [128, 1], F32) for k in range(NJ)]
        g.ntmp = al("ntmp", [128, T], F32)
        e4 = o[0]
        if P:
            g.xs1 = [al(f"xs1_{k}", [128, D], F32, at=PH + 53248 if k == 0 else None) for k in range(4)]
            g.xs0 = [al(f"xs0_{k}", [128, D], F32) for k in range(4)]
            e4 = max(e4, o[0])
        if not P:
            o[0] = max(e1, e2, e3, e4)
            g.xs0 = g.xs1 = [al("xss", [16, D], F32)]
            g.G1s = al("G1s", [16, D], F32)
            g.G2s = al("G2s", [16, D], F32)
            e4 = o[0]
        assert max(e1, e2, e3, e4) <= fw.ARENA, (e1, e2, e3, e4)
        g.end = max(e1, e2, e3, e4)
        return g

    gs = mk_group("s", 0)
    gp = mk_group("p", 0)

    so = gs.end
    modrows = fw.sb("modrows", [17, 6144], F32, so)
    bada = fw.sb("bada", [17, 6144], F32, so + 24576)
    gpostbc = fw.sb("gpostbc", [128, 2, D], F32, so + 49152)
    assert so + 49152 + 8192 + 1024 <= fw.ARENA
    cTf = fw.sb("cTf", [128, 8 * 17], F32, so + 57344)
    NST = 10
    gs.st = [fw.sb(f"st{k}", [128, 2, 512], F32, so + k * 4096) for k in range(NST)]
    gs.stb = [fw.sb(f"stb{k}", [128, 2, 512], BF16, so + NST * 4096 + k * 2048) for k in range(4)]
    st_issued = [0]

    def st_load_upto(n):
        while st_issued[0] < min(n, 64):
            i = st_issued[0]
            B.dma("sp", gs.st[i % NST][:, :, :],
                  DD["sret"][i % 16, i // 16].rearrange("(c p) v -> p c v", p=128))
            st_issued[0] += 1

    blocks = []

    def wblk(name, r0, nk, c0, ncols):
        blocks.append((DD[name][r0:r0 + nk * 128, c0:c0 + ncols], nk, ncols))
        return len(blocks) - 1

    issued = [0]

    blk_pb = {}

    def issue_upto(n):
        while issued[0] < min(n, len(blocks)):
            b = issued[0]
            src, nk, ncols = blocks[b]
            slot = slots[b % NS]
            pinfo = blk_pb.get(b)
            if pinfo is None or pinfo[0] == 0:
                B.dma("pool", slot[:, 0:nk, 0:ncols], src.rearrange("(k p) n -> p k n", p=128))
            else:
                B.dma("pool", slot[:, :, :].m(lambda a: a.rearrange("p k n -> p (k n)")), wscr[pinfo[1]],
                      xr=[wscr_tt[pinfo[1]]])
            issued[0] += 1

    def cache_block(b):
        pinfo = blk_pb.get(b)
        slot = slots[b % NS]
        if pinfo is not None and pinfo[0] == 0:
            B.dma("sp", wscr[pinfo[1]], slot[:, :, :].m(lambda a: a.rearrange("p k n -> p (k n)")),
                  xw=[wscr_tt[pinfo[1]]])

    def use_block(b, cache=True, keep=0):
        issue_upto(b + NS - keep)
        slot = slots[b % NS]
        pinfo = blk_pb.get(b)
        if cache and pinfo is not None and pinfo[0] == 0:
            B.dma("sp", wscr[pinfo[1]], slot[:, :, :].m(lambda a: a.rearrange("p k n -> p (k n)")),
                  xw=[wscr_tt[pinfo[1]]])
        return slot

    sched = {"ada": [wblk("w_ada", 0, 8, c * 512, 512) for c in range(12)]}
    passes = [("p", t) for t in range(4)] + [("s", 0)]
    for pi, _ in enumerate(passes):
        s = {}
        nb0 = len(blocks)
        s["gc"] = [wblk("w_in", 0, 8, 9216 + b * 512, 512) for b in range(2)]
        s["au"] = []
        for b in range(2):
            s["au"].append(wblk("w_in", 0, 8, 7168 + b * 512, 512))
            s["au"].append(wblk("w_in", 0, 8, 6144 + b * 512, 512))
        s["q"] = [wblk("w_in", 0, 8, b * 512, 512) for b in range(2)]
        s["k"] = [wblk("w_in", 0, 8, 1024 + b * 512, 512) for b in range(2)]
        s["wc"] = [wblk("w_conv_out", 0, 8, b * 512, 512) for b in range(2)]
        s["vg"] = []
        for h in range(4):
            s["vg"].append(wblk("w_in", 0, 8, 2048 + h * 512, 512))
            s["vg"].append(wblk("w_in", 0, 8, 4096 + h * 512, 512))
        s["gr"] = [wblk("w_in", 0, 8, 8192 + b * 512, 512) for b in range(2)]
        s["wr"] = [wblk("w_ret_out", rh * 1024, 8, ch * 512, 512) for ch in range(2) for rh in range(2)]
        s["wo"] = [wblk("w_out", 0, 8, b * 512, 512) for b in range(2)]
        s["gu"] = []
        for i in range(6):
            nc_ = 512 if i < 5 else 256
            s["gu"].append(wblk("w_gate", 0, 8, i * 512, nc_))
            s["gu"].append(wblk("w_up", 0, 8, i * 512, nc_))
        s["wd"] = [wblk("w_down", rb * 1024, 8 if rb < 2 else 6, ch * 512, 512)
                   for ch in range(2) for rb in range(3)]
        sched[pi] = s
        assert len(blocks) - nb0 == NBP
        for i in range(NBP):
            blk_pb[nb0 + i] = (pi, i)

    sp = "sp"
    B.dma(sp, pp[:, :], DD["pp"])
    B.dma(sp, identf[:, :], DD["ident"])
    B.dma("pool", identb[:, :], DD["ident"])
    B.dma(sp, dm[:, :], DD["dm"])
    B.dma(sp, kdec[:, :], DD["kdec"])
    B.dma(sp, diag16[:, :], DD["diag16"])
    B.dma(sp, sel[:, :], DD["sel"])
    B.dma(sp, cTf[:, :], DD["cT"])
    B.dma(sp, tabs[:, :, :], DD["tabs"])
    B.dma(sp, bada[:, :], DD["b_ada"].partition_broadcast(17))
    for i in range(2):
        B.dma(sp, gpostbc[:, i, :], DD["gpost"][i:i + 1, :].partition_broadcast(128))
    B.memset("pool", onesb[:, :], 1.0)
    B.memset("pool", halo[:, :, :], 0.0)
    B.memset("pool", B.epsc[:, :], EPS)
    for i in range(8):
        B.memset("pool", S32[i][:, :], 0.0)
        B.memset("pool", S16[i][:, :], 0.0)
    issue_upto(NS - 1)
    B.dma(sp, DD["css"][:, 0:29, :], DD["sconv"][:, 1:30, :])

    for rt in range(4):
        strow = gs.strow[0]
        B.dma(sp, strow[:, :], DD["sconv"][rt * 4:(rt + 1) * 4, :, :].rearrange("s w d -> (s w) d"))
        for c in range(8):
            bank = fw.ps()
            B.tr(bank[:, 0:120], strow[:, c * 128:(c + 1) * 128], identf[0:120, 0:120])
            B.tt("dve", gs.cvtmp[:, :].m(lambda a: a.rearrange("p (s w) -> p s w", w=30)),
                 bank[:, 0:120].m(lambda a: a.rearrange("p (s w) -> p s w", w=30)),
                 cwT.m(lambda a: a[:, c, 0:30].unsqueeze(1).broadcast_to([128, 4, 30])), ALU.mult)
            B.fw.emit("dve", lambda e, c=c, rt=rt: e.reduce_sum(
                out=cvs_p[:, c, rt * 4:(rt + 1) * 4].ap,
                in_=gs.cvtmp[:, :].ap.rearrange("p (s w) -> p s w", w=30),
                axis=mybir.AxisListType.X), reads=[gs.cvtmp], writes=[cvs_p])
    B.act(scT[:, :], cTf[:, :], AF.Silu)
    for c, b in enumerate(sched["ada"]):
        slot = use_block(b)
        bank = fw.ps()
        for k in range(8):
            B.mm(bank[0:17, :], scT[:, k * 17:(k + 1) * 17], slot[:, k, :], k == 0, k == 7)
        B.tt("dve", modrows[:, c * 512:(c + 1) * 512], bank[0:17, :], bada[:, c * 512:(c + 1) * 512], ALU.add)
    for half in range(2):
        bank = fw.ps()
        for q in range(24):
            cq = half * 24 + q
            B.tr(bank[:, q * 17:(q + 1) * 17], modrows[0:17, cq * 128:(cq + 1) * 128], identf[0:17, 0:17])
        B.cp("dve", modT[:, half * 24:(half + 1) * 24, :],
             bank[:, 0:408].m(lambda a: a.rearrange("p (q s) -> p q s", s=17)))
    for (AT, c0, gp_) in ((A1T, 8, gpre1), (A2T, 32, gpre2)):
        B.ts("dve", AT[:, :, :], modT[:, c0:c0 + 8, :], 1.0, None, ALU.add)
        B.tt("dve", AT[:, :, :], AT[:, :, :],
             gp_.m(lambda a: a.unsqueeze(2).broadcast_to([128, 8, 17])), ALU.mult)
    for (Gbc, Gs, c0, gi) in ((G1bc, gs.G1s, 2048, 0), (G2bc, gs.G2s, 5120, 1)):
        for ch in range(2):
            bank = fw.ps()
            B.mm(bank[:, :], sel[:, :], modrows[:, c0 + ch * 512:c0 + (ch + 1) * 512], True, True)
            B.tt("dve", Gbc[:, ch * 512:(ch + 1) * 512], bank[:, :], gpostbc[:, gi, ch * 512:(ch + 1) * 512],
                 ALU.mult)
        B.tt("dve", Gs[:, :], modrows[0:16, c0:c0 + 1024], gpostbc[0:16, gi, :], ALU.mult)
        B.dma(sp, gscr[gi], Gs[:, :], xw=[gscr_tt[gi]])

    def norm_to_hT(g, AT, BT, xs):
        T, CS, NJ = g.T, g.CS, g.NJ
        tbk = [fw.ps(hold=True) for _ in range(4)]
        tbb = [bf(b_[:, :]) for b_ in tbk]
        for j in range(NJ):
            B.act(g.xn[j][:, :], xs[j][0:CS, :], AF.Square, accum=g.ssj[j][0:CS, :])
            B.rsqrt(g.rsj[j][0:CS, :], g.ssj[j][0:CS, :], 1.0 / D, CS)
            B.ts("dve", g.xn[j][:, :], xs[j][0:CS, :], g.rsj[j][0:CS, 0:1], None, ALU.mult)
            for k in range(8):
                c0 = (k % 2) * 512 + j * CS
                B.tr(tbb[k // 2].m(lambda a: a[:, c0:c0 + CS]), g.xn[j][:, k * 128:(k + 1) * 128],
                     identb[0:CS, 0:CS])
        for k in range(8):
            c0 = (k % 2) * 512
            src = tbb[k // 2].m(lambda a: a[:, c0:c0 + T])
            if g.kind == "p":
                B.act(g.hT[k][:, :], src, AF.Identity, scale=AT[:, k, 16:17], bias=BT[:, k, 16:17])
            else:
                B.tt("dve", g.ntmp[:, :], src, AT[:, k, 0:16], ALU.mult)
                B.tt("dve", g.hT[k][:, :], g.ntmp[:, :], BT[:, k, 0:16], ALU.add)
        for b_ in tbk:
            fw.release(b_)

    def proj_fm_kouter(g, slot, nk, rhs, nm, evac):
        banks = [fw.ps(hold=True) for _ in range(nm)]
        for k in range(nk):
            for m in range(nm):
                B.mm(banks[m][:, 0:g.T], slot[:, k, m * 128:(m + 1) * 128], rhs(k), k == 0, k == nk - 1)
        for m in range(nm):
            evac(m, banks[m][:, 0:g.T])
            fw.release(banks[m])

    def proj_fm(g, slot, nk, rhs, nm, evac, banks=None, first=True, last=True):
        for m in range(nm):
            bank = banks[m] if banks is not None else fw.ps()
            for k in range(nk):
                B.mm(bank[:, 0:g.T], slot[:, k, m * 128:(m + 1) * 128], rhs(k),
                     first and k == 0, last and k == nk - 1)
            if last and evac is not None:
                evac(m, bank[:, 0:g.T])

    def proj_tm(g, slot, nk, lhs, ncols, evac, banks=None, first=True, last=True):
        CS = g.CS
        for j in range(g.NJ):
            bank = banks[j] if banks is not None else fw.ps()
            for k in range(nk):
                B.mm(bank[0:CS, 0:ncols], lhs(k, j), slot[:, k, 0:ncols],
                     first and k == 0, last and k == nk - 1)
            if last and evac is not None:
                evac(j, bank[0:CS, 0:ncols])

    def resid(g, src, ssx, Gp, Gs_, after=None):
        CS, NJ = g.CS, g.NJ
        Gv = Gp[:, :] if g.kind == "p" else Gs_[:, :]
        for j in range(NJ):
            B.act(g.xn[j][:, :], src[j][:, :], AF.Square, accum=g.ssj[j][0:CS, :])
            B.rsqrt(g.rsj[j][0:CS, :], g.ssj[j][0:CS, :], 1.0 / D, CS)
            B.stt("dve", src[j][:, :], src[j][:, :], g.rsj[j][0:CS, 0:1], Gv, ALU.mult, ALU.mult)
            B.tt("dve", g.xs1[j][0:CS, :], g.xs1[j][0:CS, :], src[j][:, :], ALU.add)
            if after is not None:
                after(j)

    def rotary(g, bA, bB, cos, sin, outA, outB):
        t = g.tmp
        B.tt("dve", t[0][:, :], bA, cos, ALU.mult)
        B.tt("dve", t[1][:, :], bB, sin, ALU.mult)
        B.tt("pool", outA[:, :], t[0][:, :], t[1][:, :], ALU.subtract)
        B.tt("dve", t[2][:, :], bB, cos, ALU.mult)
        B.tt("dve", t[3][:, :], bA, sin, ALU.mult)
        B.tt("pool", outB[:, :], t[2][:, :], t[3][:, :], ALU.add)

    def gn_gate(g, ybank, sgv, rn, r):
        CS = g.CS
        B.fw.emit("dve", lambda e: e.bn_stats(out=g.st6[0:CS, :].ap, in_=ybank.ap),
                  reads=[ybank.tt], writes=[g.st6])
        B.fw.emit("dve", lambda e: e.bn_aggr(out=g.mv[0:CS, :].ap, in_=g.st6[0:CS, :].ap),
                  reads=[g.st6], writes=[g.mv])
        B.rsqrt(g.gsd[0:CS, :], g.mv[0:CS, 1:2], 1.0, CS)
        B.stt("dve", g.gnb[0:CS, :], g.mv[0:CS, 0:1], -1.0, g.gsd[0:CS, :], ALU.mult, ALU.mult)
        B.act(rn[:, :], ybank, AF.Identity, scale=g.gsd[0:CS, 0:1], bias=g.gnb[0:CS, 0:1])
        B.tt("pool", r[:, :], rn[:, :], sgv, ALU.mult)

    def sq_evac(g, dst, bankv, acc):
        B.act(dst, bankv, AF.Copy)

    class _Stop(Exception):
        pass

    def front(g, t):
        if g.kind == "p":
            for j in range(4):
                B.dma(sp, g.xs0[j][:, :], DD["xp"][t * 512 + j * 128:t * 512 + (j + 1) * 128, :])
        else:
            B.dma(sp, g.xs0[0][:, :], DD["xs"])
            B.dma(sp, g.G1s[:, :], gscr[0], xr=[gscr_tt[0]])
            B.dma(sp, g.G2s[:, :], gscr[1], xr=[gscr_tt[1]])
            st_load_upto(NST - 1)
        norm_to_hT(g, A1T, Bv1T, g.xs0)

    def run_pass(pi, g):
        try:
            run_pass_(pi, g)
        except _Stop:
            pass

    def run_pass_(pi, g):
        import os
        kph = float(os.environ.get("KPH", "99")) if pi == int(os.environ.get("KSTOP", "99")) - 1 else 99
        P = g.kind == "p"
        T, CS, NJ = g.T, g.CS, g.NJ
        t = g.tile
        s = sched[pi]
        first_tile = P and t == 0
        last_tile = P and t == 3
        hTk = lambda k: g.hT[k][:, :]
        hTkj = lambda k, j: g.hT[k][:, j * CS:(j + 1) * CS]

        if (not P) or t == 0:
            front(g, t)

        if kph <= 0:
            raise _Stop()
        for b in range(2):
            slot = use_block(s["gc"][b])
            proj_fm(g, slot, 8, hTk, 4, lambda m, bv: B.act(g.sgc[b * 4 + m][:, :], bv, AF.Sigmoid))
        if P:
            for c in range(8):
                B.cp("pool", g.glu[c][:, 0:30], halo[:, c, 0:30])
        for b in range(2):
            slot = use_block(s["au"][2 * b])
            proj_fm(g, slot, 8, hTk, 4, lambda m, bv: B.act(g.siga[m][:, :], bv, AF.Sigmoid))
            slot = use_block(s["au"][2 * b + 1])

            def ev_u(m, bv):
                c = b * 4 + m
                if P:
                    B.tt("dve", g.glu[c][:, 30:30 + T], bv, g.siga[m][:, :], ALU.mult)
                    if last_tile:
                        B.tt("dve", g.gtail[:, c, 0:30], bv.m(lambda a: a[:, 482:512]),
                             g.siga[m][:, 482:512], ALU.mult)
                else:
                    B.tt("dve", g.glu[c][:, :], bv, g.siga[m][:, :], ALU.mult)
            proj_fm(g, slot, 8, hTk, 4, ev_u)

        bsum = fw.ps(hold=True)
        bsq = fw.ps(hold=True)
        if P:
            for c in range(8):
                dg = g.diag[c % 2]
                B.tt("pool", dg[:, :, :], identb[:, :].m(lambda a: a.unsqueeze(1).broadcast_to([128, 31, 128])),
                     cwT.m(lambda a: a[:, c, :].unsqueeze(2).broadcast_to([128, 31, 128])), ALU.mult)
                bank = fw.ps()
                for w in range(31):
                    B.mm(bank[:, :], dg[:, w, :], g.glu[c][:, w:w + 512], w == 0, w == 30)
                B.act(g.cvT[c][:, :], bank[:, :], AF.Identity, bias=convb.m(lambda a: a[:, c:c + 1]))
                sqb = g.sq[c % 2]
                B.act(sqb[:, :], bank[:, :], AF.Square, bias=convb.m(lambda a: a[:, c:c + 1]))
                B.mm(bsum[:, 0:T], onesb[:, :], g.cvT[c][:, :], c == 0, c == 7)
                B.mm(bsq[:, 0:T], onesb[:, :], sqb[:, :], c == 0, c == 7)
            if not last_tile:
                for c in range(8):
                    B.cp("pool", halo[:, c, 0:30], g.glu[c][:, 512:542])
            else:
                for half in range(2):
                    bank = fw.ps()
                    for q in range(4):
                        c = half * 4 + q
                        B.tr(bank[0:30, q * 128:(q + 1) * 128], g.gtail[:, c, 0:30], identf[:, :])
                    B.cp("dve", g.ctail[0:30, half * 512:(half + 1) * 512], bank[0:30, :])
                B.dma(sp, DD["csp"], g.ctail[0:30, :])
        else:
            for c in range(8):
                B.stt("dve", g.cvs[:, c, :], g.glu[c][:, :], cwT.m(lambda a: a[:, c, 30:31]), cvs_p[:, c, :],
                      ALU.mult, ALU.add)
                B.act(g.cvT[c][:, :], g.cvs[:, c, :], AF.Identity, bias=convb.m(lambda a: a[:, c:c + 1]))
                sqb = g.sq[c % 2]
                B.act(sqb[:, :], g.cvs[:, c, :], AF.Square, bias=convb.m(lambda a: a[:, c:c + 1]))
                B.mm(bsum[:, 0:T], onesb[:, :], g.cvT[c][:, :], c == 0, c == 7)
                B.mm(bsq[:, 0:T], onesb[:, :], sqb[:, :], c == 0, c == 7)
            for half in range(2):
                bank = fw.ps()
                for q in range(4):
                    c = half * 4 + q
                    B.tr(bank[0:16, q * 128:(q + 1) * 128], g.glu[c][:, :], identf[:, :])
                B.cp("dve", g.ctail[0:16, half * 512:(half + 1) * 512], bank[0:16, :])
            B.dma(sp, DD["css"][:, 29, :], g.ctail[0:16, :])
        B.ts("dve", g.mean[:, :], bsum[:, 0:T], 1.0 / D, None, ALU.mult)
        B.tt("dve", g.msq[:, :], g.mean[:, :], g.mean[:, :], ALU.mult)
        B.stt("dve", g.rstd[:, :], bsq[:, 0:T], 1.0 / D, g.msq[:, :], ALU.mult, ALU.subtract)
        fw.release(bsum)
        fw.release(bsq)
        B.rsqrt(g.rstd[:, :], g.rstd[:, :], 1.0, 128)
        def ln_chunk(c):
            lt = g.lntmp if c % 2 == 0 else g.lntmpb
            ltb = fw.ps()
            B.tt("dve", ltb[:, 0:T], g.cvT[c][:, :], g.mean[:, :], ALU.subtract)
            B.tt("dve", lt[:, :], ltb[:, 0:T], g.rstd[:, :], ALU.mult)
            B.act(g.lnT[c][:, :], lt[:, :], AF.Silu, scale=lng.m(lambda a: a[:, c:c + 1]),
                  bias=lnb.m(lambda a: a[:, c:c + 1]))
        ln_next = [0]

        def ln_some(n):
            for _ in range(n):
                if ln_next[0] < 8:
                    ln_chunk(ln_next[0])
                    ln_next[0] += 1

        if kph <= 1:
            raise _Stop()
        tbuf = [g.tabq, g.tabk]

        def tab_load(i):
            if not P:
                return
            if i < 4:
                B.dma(sp, tbuf[i % 2][:, :, :], DD["tabq"][i, :, :, t * 512:(t + 1) * 512])
            else:
                B.dma(sp, tbuf[0][:, :, :], DD["tabk"][:, :, t * 512:(t + 1) * 512])
        if P:
            tab_load(0)
            tab_load(1)
            kcos, ksin = tbuf[0][:, 0, :], tbuf[0][:, 1, :]
        else:
            kcos, ksin = tabs[:, 0, :], tabs[:, 1, :]
        for b in range(2):
            slot = use_block(s["q"][b])
            for pr in range(2):
                h = b * 2 + pr
                if P:
                    if h >= 1:
                        tab_load(h + 1)
                    qcos, qsin = tbuf[h % 2][:, 0, :], tbuf[h % 2][:, 1, :]
                else:
                    qcos, qsin = tabs[:, 2 + 2 * h, :], tabs[:, 3 + 2 * h, :]
                ln_some(1)
                bks = [fw.ps(), fw.ps()]
                for mm_ in range(2):
                    m = pr * 2 + mm_
                    for k in range(8):
                        B.mm(bks[mm_][:, 0:T], slot[:, k, m * 128:(m + 1) * 128], hTk(k), k == 0, k == 7)
                rotary(g, bks[0][:, 0:T], bks[1][:, 0:T], qcos, qsin, g.qk[h * 2], g.qk[h * 2 + 1])
        for b in range(2):
            slot = use_block(s["k"][b])
            for pr in range(2):
                h = b * 2 + pr
                ln_some(1)
                bks = [fw.ps(), fw.ps()]
                for mm_ in range(2):
                    m = pr * 2 + mm_
                    for k in range(8):
                        B.mm(bks[mm_][:, 0:T], slot[:, k, m * 128:(m + 1) * 128], hTk(k), k == 0, k == 7)
                rotary(g, bks[0][:, 0:T], bks[1][:, 0:T], kcos, ksin, g.qk[8 + h * 2], g.qk[8 + h * 2 + 1])

        ln_some(8)
        for b in range(2):
            slot = use_block(s["wc"][b])
            proj_fm(g, slot, 8, lambda k: g.lnT[k][:, :], 4,
                    lambda m, bv: B.tt("dve", g.cpart[b * 4 + m][:, :], bv, g.sgc[b * 4 + m][:, :], ALU.mult))
        pend = []
        LAG = 4
        rcount = [0]

        def flush(n):
            while len(pend) > n:
                pend.pop(0)()

        def head_proj(h, hh):
            kT = [g.qk[8 + h * 2], g.qk[8 + h * 2 + 1]]
            slot = use_block(s["vg"][2 * h])
            proj_tm(g, slot, 8, hTkj, 512, lambda j, bv: B.act(g.vv[hh][j][:, :], bv, AF.Copy))
            slot = use_block(s["vg"][2 * h + 1])
            proj_tm(g, slot, 8, hTkj, 512, lambda j, bv: B.act(g.sgg[hh][j][:, :], bv, AF.Silu))
            bank = fw.ps()
            bb = bf(bank[:, :])
            for j in range(NJ):
                for half in range(2):
                    o0 = j * 256 + half * 128
                    B.tr(bb.m(lambda a: a[0:CS, o0:o0 + 128]), kT[half][:, j * CS:(j + 1) * CS], identb[:, :])
            kflat = g.khh[hh][:, :, :].m(lambda a: a.rearrange("p j d -> p (j d)"))
            if P:
                B.ts("dve", kflat, bb.m(lambda a: a[:, 0:1024]), kdec[:, h:h + 1], None, ALU.mult)
            else:
                B.cp("dve", kflat, bb.m(lambda a: a[0:16, 0:256]))

        pendB = []
        gnr = []
        for k in range(4):
            gnr.append(dict(st6=g.st6s[k], mv=g.mvs[k], gsd=g.gsds[k], gnb=g.gnbs[k]))

        def flushB(n):
            while len(pendB) > n:
                pendB.pop(0)()

        ccount = [0]

        def chunk(h, hh, j):
            qT = [g.qk[h * 2], g.qk[h * 2 + 1]]
            kT = [g.qk[8 + h * 2], g.qk[8 + h * 2 + 1]]
            v, sg, khat = g.vv[hh], g.sgg[hh], g.khh[hh]
            cs = slice(j * 128, (j + 1) * 128)
            n = rcount[0]
            rcount[0] += 1
            bk = fw.banks
            sb_ = bk[0]
            yb = bk[1 + n % 3]
            for half in range(2):
                B.mm(sb_[:, 0:128], kT[half][:, cs], qT[half][:, cs], half == 0, half == 1)
            Pm = g.Pm[n % 2]
            B.tt("dve", Pm[:, :], sb_[:, 0:128], dm[:, h * 128:(h + 1) * 128], ALU.mult)
            for half in range(2):
                sbk = bk[4 + half]
                B.mm(sbk[:, :], khat[:, j, half * 128:(half + 1) * 128], v[j][:, :], True, True)
            B.mm(yb[:, :], Pm[:, :], v[j][:, :], True, False)
            for half in range(2):
                B.mm(yb[:, :], qT[half][:, cs], S16[h * 2 + half][:, :], False, half == 1)
            for half in range(2):
                sbk = bk[4 + half]
                B.stt("dve", S32[h * 2 + half][:, :], S32[h * 2 + half][:, :], float(gam128[h]),
                      sbk[:, :], ALU.mult, ALU.add)
                if half == 0:
                    B.act(S16[h * 2 + half][:, :], S32[h * 2 + half][:, :], AF.Copy)
                else:
                    B.cp("pool", S16[h * 2 + half][:, :], S32[h * 2 + half][:, :])
            rn = g.rn[n % 2]
            r = g.r[n % len(g.r)]
            gb = gnr[n % 4]
            ybv = yb[:, :]
            B.fw.emit("dve", lambda e: e.bn_stats(out=gb["st6"][:, :].ap, in_=ybv.ap),
                      reads=[ybv.tt], writes=[gb["st6"]])
            B.fw.emit("dve", lambda e: e.bn_aggr(out=gb["mv"][:, :].ap, in_=gb["st6"][:, :].ap),
                      reads=[gb["st6"]], writes=[gb["mv"]])
            B.act(gb["gsd"][:, :], gb["mv"][:, 1:2], AF.Sqrt, bias=B.epsc[:, 0:1], scale=1.0)

            def B2():
                B.recip(gb["gsd"][:, :], gb["gsd"][:, :])
                B.stt("dve", gb["gnb"][:, :], gb["mv"][:, 0:1], -1.0, gb["gsd"][:, :], ALU.mult, ALU.mult)
                B.act(rn[:, :], ybv, AF.Identity, scale=gb["gsd"][:, 0:1], bias=gb["gnb"][:, 0:1])
                B.tt("pool", r[:, :], rn[:, :], sg[j][:, :], ALU.mult)

                def C():
                    tb = bk[6 + ccount[0] % 2]
                    ccount[0] += 1
                    tbb = bf(tb[:, :])
                    for q4 in range(4):
                        B.tr(tbb.m(lambda a: a[:, q4 * 128:(q4 + 1) * 128]),
                             r[:, q4 * 128:(q4 + 1) * 128], identb[:, :])
                    B.cp("dve", g.rT[h][:, :, cs],
                         tbb.m(lambda a: a[:, 0:512].rearrange("p (q t) -> p q t", q=4)))
                pend.append(C)
            pendB.append(B2)
            flushB(1)
            flush(LAG)

        if P:
            for hp in range(0, NH, 2):
                flushB(0)
                for hh in range(2):
                    head_proj(hp + hh, hh)
                for j in range(NJ):
                    for hh in range(2):
                        chunk(hp + hh, hh, j)
                if last_tile:
                    for hh in range(2):
                        h = hp + hh
                        for half in range(2):
                            B.dma(sp, DD["rsp"][h, half * 128:(half + 1) * 128, :], S32[h * 2 + half][:, :])
            bkx = fw.banks
            grb = [[bkx[0], bkx[4], bkx[5], bkx[6]], [bkx[7], bkx[0], bkx[4], bkx[5]]]
            for b in range(2):
                slot = use_block(s["gr"][b])
                proj_fm(g, slot, 8, hTk, 4, lambda m, bv: B.act(g.sgr[b * 4 + m][:, :], bv, AF.Sigmoid),
                        banks=grb[b])
            flushB(0)
            flush(0)
        else:
            for h in range(NH):
                qT = [g.qk[h * 2], g.qk[h * 2 + 1]]
                head_proj(h, 0)
                for c in range(2):
                    B.tt("dve", g.qTm[:, c, :, :],
                         qT[c][:, :].m(lambda a: a.unsqueeze(1).broadcast_to([128, 16, 16])),
                         diag16[:, :].m(lambda a: a.rearrange("p (s t) -> p s t", t=16)), ALU.mult)
                ob = fw.ps(hold=True)
                pend_o = []

                def flush_o(n):
                    while len(pend_o) > n:
                        pend_o.pop(0)()
                for smp in range(16):
                    gi = h * 16 + smp
                    st_load_upto(gi + NST - 1)
                    st = g.st[gi % NST]
                    km = g.km[smp % 2]
                    B.ts("dve", km[:, :], g.khh[0][:, 0, :], identf[0:16, smp:smp + 1], None, ALU.mult)
                    for c in range(2):
                        kb = fw.ps()
                        B.mm(kb[:, :], km[:, c * 128:(c + 1) * 128], g.vv[0][0][:, :], True, True)
                        B.stt("dve", st[:, c, :], st[:, c, :], float(gam[h]), kb[:, :], ALU.mult, ALU.add)
                    stb = g.stb[gi % 4]
                    B.act(stb[:, :, :], st[:, :, :], AF.Copy)
                    B.dma(sp, DD["rss"][smp, h].rearrange("(c p) v -> p c v", p=128), st[:, :, :])

                    def mk_o(smp=smp, stb=stb):
                        def o_():
                            for c in range(2):
                                B.mm(ob[0:16, :], g.qTm[:, c, smp, :], stb[:, c, :], smp == 0 and c == 0,
                                     smp == 15 and c == 1)
                        return o_
                    pend_o.append(mk_o())
                    flush_o(2)
                flush_o(0)
                rn, r = g.rn[h % 2], g.r[h % 2]
                gn_gate(g, ob[0:16, :], g.sgg[0][0][:, :], rn, r)
                fw.release(ob)
                tb = fw.ps()
                tbb = bf(tb[:, :])
                for q4 in range(4):
                    B.tr(tbb.m(lambda a: a[:, q4 * 16:(q4 + 1) * 16]), r[:, q4 * 128:(q4 + 1) * 128],
                         identb[0:16, 0:16])
                B.act(g.rT[h][:, :, :], tbb.m(lambda a: a[:, 0:64].rearrange("p (q t) -> p q t", q=4)), AF.Copy)

        if kph <= 2:
            raise _Stop()
        if P:
            for j in range(4):
                B.dma(sp, g.xs1[j][:, :], DD["xp"][t * 512 + j * 128:t * 512 + (j + 1) * 128, :])
        if not P:
            for b in range(2):
                slot = use_block(s["gr"][b])
                proj_fm(g, slot, 8, hTk, 4, lambda m, bv: B.act(g.sgr[b * 4 + m][:, :], bv, AF.Sigmoid))
        if kph <= 2.2:
            raise _Stop()
        for ch in range(2):
            banks = [fw.ps(hold=True) for _ in range(4)]
            for rh in range(2):
                slot = use_block(s["wr"][ch * 2 + rh], cache=False)
                if pi == 0:
                    for k8 in range(8):
                        B.act(slot[:, k8, :], slot[:, k8, :], AF.Copy,
                              scale=gng.m(lambda a: a[:, rh * 8 + k8:rh * 8 + k8 + 1]))
                    cache_block(s["wr"][ch * 2 + rh])

                def ev_r(m, bv):
                    c = ch * 4 + m
                    B.tt("dve", g.mtmp[:, :], bv, g.sgr[c][:, :], ALU.mult)
                    B.tt("dve", g.merged[c][:, :], g.mtmp[:, :], g.cpart[c][:, :], ALU.add)
                proj_fm(g, slot, 8, lambda k: g.rT[(rh * 8 + k) // 4][:, (rh * 8 + k) % 4, :], 4, ev_r,
                        banks=banks, first=rh == 0, last=rh == 1)
            for bk in banks:
                fw.release(bk)
        if kph <= 2.5:
            raise _Stop()
        wslots = [use_block(s["wo"][0]), use_block(s["wo"][1], keep=1)]
        for j in range(NJ):
            for ch in range(2):
                bank = fw.ps()
                for k in range(8):
                    B.mm(bank[0:CS, 0:512], g.merged[k][:, j * CS:(j + 1) * CS], wslots[ch][:, k, 0:512],
                         k == 0, k == 7)
                sq_evac(g, g.mix[j][:, ch * 512:(ch + 1) * 512], bank[0:CS, 0:512], None)
        if kph <= 2.8:
            raise _Stop()
        resid(g, g.mix, g.ssm, G1bc, None if P else g.G1s)

        if kph <= 3:
            raise _Stop()
        norm_to_hT(g, A2T, Bv2T, g.xs1)
        for i in range(6):
            nm = 4 if i < 5 else 2
            slot = use_block(s["gu"][2 * i])
            (proj_fm_kouter if i == 0 else proj_fm)(g, slot, 8, hTk, nm,
                                                    lambda m, bv: B.act(g.sgt[m][:, :], bv, AF.Silu))
            slot = use_block(s["gu"][2 * i + 1])
            proj_fm(g, slot, 8, hTk, nm,
                    lambda m, bv: B.tt("dve", g.aT[i * 4 + m][:, :], bv, g.sgt[m][:, :], ALU.mult))
        if P and t < 3:
            front(g, t + 1)
        for ch in range(2):
            banks = [fw.ps(hold=True) for _ in range(NJ)]
            for rb in range(3):
                slot = use_block(s["wd"][ch * 3 + rb])
                nk = 8 if rb < 2 else 6
                proj_tm(g, slot, nk, lambda k, j: g.aT[rb * 8 + k][:, j * CS:(j + 1) * CS], 512,
                        lambda j, bv: sq_evac(g, g.ffo[j][:, ch * 512:(ch + 1) * 512], bv,
                                              g.ssm[0:CS, j * 2 + ch:j * 2 + ch + 1]),
                        banks=banks, first=rb == 0, last=rb == 2)
            for bk in banks:
                fw.release(bk)
        if P:
            resid(g, g.ffo, g.ssm, G2bc, None,
                  after=lambda j: B.dma(sp, DD["yp"][t * 512 + j * 128:t * 512 + (j + 1) * 128, :],
                                        g.xs1[j][:, :]))
        else:
            resid(g, g.ffo, g.ssm, G2bc, g.G2s, after=lambda j: B.dma(sp, DD["ys"], g.xs1[0][:, :]))

    import os
    kstop = int(os.environ.get("KSTOP", "99"))
    for pi, (kind, t) in enumerate(passes):
        if pi >= kstop:
            break
        if kind == "s":
            run_pass(pi, gs)
        else:
            gp.tile = t
            run_pass(pi, gp)

    fw.final_wait("sp")
    with ExitStack() as st:
        fw.generate(st)
    return B.nc


_CACHE = {}


def _consts():
    if "c" in _CACHE:
        return _CACHE["c"]
    lg = np.log(1.0 - np.exp(np.linspace(np.log(1.0 / 32.0), np.log(1.0 / 512.0), NH)))
    inv_freq = (np.float32(10000.0) ** (-np.arange(0, 256, 2, dtype=np.float32) / np.float32(256.0)))
    inv_freq = inv_freq.astype(np.float32)
    pos = np.arange(2048, dtype=np.float32)
    ang = (inv_freq[:, None] * pos[None, :]).astype(np.float32).astype(np.float64)
    cos, sin = np.cos(ang), np.sin(ang)
    tabk = np.stack([cos, sin], axis=1).astype(np.float32)
    i_in = (np.arange(2048) % 128).astype(np.float64)
    tabq = np.zeros((4, 128, 2, 2048), np.float32)
    for h in range(NH):
        dec = np.exp(lg[h] * (i_in + 1.0)) / 16.0
        tabq[h, :, 0, :] = cos * dec[None, :]
        tabq[h, :, 1, :] = sin * dec[None, :]
    angs = (inv_freq * np.float32(16384.0)).astype(np.float32).astype(np.float64)
    tabs = np.zeros((128, 10, 16), np.float32)
    tabs[:, 0, :] = np.cos(angs)[:, None]
    tabs[:, 1, :] = np.sin(angs)[:, None]
    for h in range(NH):
        tabs[:, 2 + 2 * h, :] = (np.cos(angs) / 16.0)[:, None]
        tabs[:, 3 + 2 * h, :] = (np.sin(angs) / 16.0)[:, None]
    j = np.arange(128, dtype=np.float64)
    dm = np.zeros((128, 4, 128), np.float32)
    kdec = np.zeros((128, 4), np.float32)
    mask = (j[None, :] >= j[:, None])
    for h in range(NH):
        dm[:, h, :] = np.exp(-lg[h] * (j + 1.0))[:, None] * mask
        kdec[:, h] = np.exp(lg[h] * (127.0 - j))
    sel = np.zeros((17, 128), np.float32)
    sel[16, :] = 1.0
    diag16 = np.broadcast_to(np.eye(16, dtype=np.float32).reshape(1, 256), (128, 256)).copy()
    c = dict(tabk=tabk, tabq=tabq, tabs=tabs, dm=dm.reshape(128, 512), kdec=kdec,
             ident=np.eye(128, dtype=np.float32), sel=sel, diag16=diag16)
    _CACHE["c"] = c
    return c


def _fm(v):
    return np.ascontiguousarray(v.reshape(-1, 128).T)


def kernel(x_prompt, x_sample, c_prompt, c_sample, state_ret, state_conv, w_in, w_ada, b_ada,
           g_pre1, g_post1, g_pre2, g_post2, conv_w, conv_b, conv_ln_g, conv_ln_b, w_conv_out,
           ret_gn_g, w_ret_out, w_out, w_ffn_gate, w_ffn_up, w_ffn_down):
    f = lambda a: np.ascontiguousarray(np.asarray(a, dtype=np.float32))
    x_prompt, x_sample, c_prompt, c_sample = f(x_prompt), f(x_sample), f(c_prompt), f(c_sample)
    state_ret, state_conv = f(state_ret), f(state_conv)
    cst = _consts()
    if "nc" not in _CACHE:
        _CACHE["nc"] = build_program()
    nc = _CACHE["nc"]
    cw = f(conv_w)[0]
    cwT = np.ascontiguousarray(cw.T.reshape(8, 128, 31).transpose(1, 0, 2)).reshape(128, 248)
    pp = np.concatenate([_fm(f(g_pre1)[0]), _fm(f(g_pre2)[0]), _fm(f(conv_b)[0]), _fm(f(conv_ln_g)[0]),
                         _fm(f(conv_ln_b)[0]), _fm(f(ret_gn_g)[0]), cwT], axis=1).astype(np.float32)
    gpost = np.stack([f(g_post1)[0], f(g_post2)[0]], axis=0)
    shared = dict(w_in=f(w_in)[0], w_ada=f(w_ada)[0], b_ada=f(b_ada), w_conv_out=f(w_conv_out)[0],
                  w_ret_out=f(w_ret_out)[0], w_out=f(w_out)[0], w_gate=f(w_ffn_gate)[0],
                  w_up=f(w_ffn_up)[0], w_down=f(w_ffn_down)[0], pp=np.ascontiguousarray(pp), gpost=gpost,
                  **cst)
    in_maps = []
    for i in range(8):
        call = np.concatenate([c_sample[16 * i:16 * i + 16], c_prompt[i:i + 1]], axis=0)
        cT = np.ascontiguousarray(call.T.reshape(8, 128, 17).transpose(1, 0, 2)).reshape(128, 136)
        m = dict(shared)
        m.update(xp=x_prompt[i], xs=np.ascontiguousarray(x_sample[16 * i:16 * i + 16, 0, :]), cT=cT,
                 sret=state_ret[0, 16 * i:16 * i + 16], sconv=state_conv[0, 16 * i:16 * i + 16])
        in_maps.append(m)
    import os
    ncores = int(os.environ.get("NCORES", "8"))
    res = run_bass_kernel_spmd(nc, in_maps[:ncores], core_ids=list(range(ncores)))
    R = list(res.results) + [res.results[0]] * (8 - ncores)
    y_prompt = np.stack([R[i]["yp"] for i in range(8)], axis=0)
    y_sample = np.concatenate([R[i]["ys"] for i in range(8)], axis=0)[:, None, :]
    rsp = np.stack([R[i]["rsp"] for i in range(8)], axis=0)[None]
    rss = np.concatenate([R[i]["rss"] for i in range(8)], axis=0)[None]
    csp = np.stack([R[i]["csp"] for i in range(8)], axis=0)[None]
    css = np.concatenate([R[i]["css"] for i in range(8)], axis=0)[None]
    return (y_prompt.astype(np.float32), y_sample.astype(np.float32), rsp.astype(np.float32),
            rss.astype(np.float32), csp.astype(np.float32), css.astype(np.float32))
```

```python
from contextlib import ExitStack
import numpy as np
import concourse.bass as bass
import concourse.mybir as mybir
from concourse.bass_utils import run_bass_kernel_spmd

F32 = mybir.dt.float32
BF16 = mybir.dt.bfloat16
AF = mybir.ActivationFunctionType
ALU = mybir.AluOpType
_DT_SIZE = {F32: 4, BF16: 2}

D = 1024
NH = 4
EPS = 1e-6
NS = 7
NPP = 8 * 5 + 16 + 8 * 31


class View:
    def __init__(self, tt, ap):
        self.tt = tt
        self.ap = ap

    def m(self, f):
        return View(self.tt, f(self.ap))

    def __getitem__(self, k):
        return View(self.tt, self.ap[k])


class TT:
    def __init__(self, name, space, lo, hi, handle):
        self.name = name
        self.space = space
        self.lo = lo
        self.hi = hi
        self.h = handle
        self.dsem = None
        self.dcount = 0

    def __getitem__(self, k):
        return View(self, self.h[k])


class Op:
    __slots__ = ("eng", "fn", "waits", "signal", "sigval", "is_dma", "dtok", "idx", "lbl")


class FW:
    ALLENG = ("pe", "act", "dve", "pool", "sp")

    def __init__(self, nc):
        self.nc = nc
        self.ops = {e: [] for e in self.ALLENG}
        self.recs = []
        self.waited = {e: {} for e in self.ALLENG}
        self.dma_sems = []
        self.uid = 0
        self.ARENA = 204800
        arena = nc.alloc_sbuf_tensor("arena", [128, self.ARENA // 4], F32)
        self.base = nc.lookup_mloc(arena).addr
        self.banks = []
        for i in range(8):
            h = nc.alloc_psum_tensor(f"psb{i}", [128, 512], F32)
            self.banks.append(TT(f"psb{i}", "ps", i * 2048, (i + 1) * 2048, h))
        self.held = set()
        self.rr = 0

    def sb(self, name, shape, dtype, off):
        nbytes = int(np.prod(shape[1:])) * _DT_SIZE[dtype]
        assert off % 4 == 0 and off + nbytes <= self.ARENA, (name, off, nbytes)
        self.uid += 1
        h = self.nc.alloc_sbuf_tensor_at(f"{name}_{self.uid}", list(shape), dtype,
                                         offset=self.base + off)
        return TT(name, "sb", off, off + nbytes, h)

    def ps(self, hold=False):
        for _ in range(8):
            b = self.banks[self.rr]
            self.rr = (self.rr + 1) % 8
            if b not in self.held:
                if hold:
                    self.held.add(b)
                return b
        raise RuntimeError("no psum bank")

    def release(self, b):
        self.held.discard(b)

    def _deps(self, reads, writes):
        deps = []
        for t in reads:
            for r in self.recs:
                if r[3] == "w" and r[0] == t.space and r[1] < t.hi and t.lo < r[2]:
                    deps.append(r[4])
        for t in writes:
            for r in self.recs:
                if r[0] == t.space and r[1] < t.hi and t.lo < r[2]:
                    deps.append(r[4])
        return deps

    def _record(self, tok, reads, writes):
        for t in writes:
            self.recs = [r for r in self.recs
                         if not (r[0] == t.space and t.lo <= r[1] and r[2] <= t.hi)]
            self.recs.append([t.space, t.lo, t.hi, "w", tok])
        for t in reads:
            if tok[0] == "c":
                found = False
                for r in self.recs:
                    if (r[3] == "r" and r[0] == t.space and r[1] == t.lo and r[2] == t.hi
                            and r[4][0] == "c" and r[4][1].eng == tok[1].eng):
                        r[4] = tok
                        found = True
                        break
                if found:
                    continue
            self.recs.append([t.space, t.lo, t.hi, "r", tok])

    def _mk_waits(self, eng, deps):
        best = {}
        for d in deps:
            if d[0] == "c":
                o = d[1]
                if o.eng == eng and eng == "pe":
                    continue
                key = ("c", o.eng)
                if key not in best or best[key][1].idx < o.idx:
                    best[key] = d
            else:
                key = ("d", id(d[1]))
                if key not in best or best[key][2] < d[2]:
                    best[key] = d
        waits = []
        w = self.waited[eng]
        for key, d in best.items():
            v = d[1].idx if d[0] == "c" else d[2]
            if w.get(key, -1) >= v:
                continue
            w[key] = v
            if d[0] == "c":
                d[1].signal = True
            waits.append(d)
        return waits

    def _newop(self, eng, fn):
        op = Op()
        op.eng = eng
        op.fn = fn
        op.signal = False
        op.sigval = None
        op.is_dma = False
        op.dtok = None
        op.idx = len(self.ops[eng])
        op.lbl = getattr(self, "lbl", "")
        return op

    def emit(self, eng, fn, reads=(), writes=()):
        op = self._newop(eng, fn)
        op.waits = self._mk_waits(eng, self._deps(reads, writes))
        self.ops[eng].append(op)
        self._record(("c", op), reads, writes)
        return op

    def dma(self, queue, fn, semtt, reads=(), writes=()):
        op = self._newop(queue, fn)
        op.is_dma = True
        if semtt.dsem is None:
            semtt.dsem = True
            self.dma_sems.append(semtt)
        semtt.dcount += 16
        tok = ("d", semtt, semtt.dcount)
        op.dtok = tok
        op.waits = self._mk_waits(queue, self._deps(reads, writes))
        self.ops[queue].append(op)
        self._record(tok, reads, writes)
        return op

    def final_wait(self, queue="sp"):
        op = self._newop(queue, None)
        op.waits = self._mk_waits(queue, [r[4] for r in self.recs])
        self.ops[queue].append(op)

    def generate(self, stack):
        nc = self.nc
        esem = {e: stack.enter_context(nc.semaphore(f"s_{e}")) for e in self.ALLENG}
        for i, t in enumerate(self.dma_sems):
            t.dsem = stack.enter_context(nc.semaphore(f"d_{i}"))
        for e in self.ALLENG:
            c = 0
            for op in self.ops[e]:
                if op.signal:
                    c += 1
                    op.sigval = c
        block = stack.enter_context(nc.Block())
        engmap = {"pe": block.tensor, "act": block.scalar, "dve": block.vector,
                  "pool": block.gpsimd, "sp": block.sync}

        def run(e):
            def body(engine):
                for op in self.ops[e]:
                    for d in op.waits:
                        if d[0] == "c":
                            engine.wait_ge(esem[d[1].eng], d[1].sigval)
                        else:
                            engine.wait_ge(d[1].dsem, d[2])
                    if op.fn is None:
                        continue
                    ins = op.fn(engine)
                    if op.is_dma:
                        ins.then_inc(op.dtok[1].dsem, 16)
                    elif op.signal:
                        ins.then_inc(esem[e], 1)
            return body

        for e in self.ALLENG:
            engmap[e](run(e))


def _tts(*vs):
    return [v.tt for v in vs if isinstance(v, View)]


def _a(v):
    return v.ap if isinstance(v, View) else v


class Builder:
    def __init__(self):
        self.nc = bass.Bass("TRN2", target_bir_lowering=False)
        self.fw = FW(self.nc)
        self.D = {}

    def din(self, name, shape):
        self.D[name] = self.nc.dram_tensor(name, list(shape), F32, kind="ExternalInput").ap()

    def dout(self, name, shape):
        self.D[name] = self.nc.dram_tensor(name, list(shape), F32, kind="ExternalOutput").ap()

    def mm(self, out, lhsT, rhs, start, stop):
        self.fw.emit("pe", lambda e: e.matmul(out.ap, lhsT=lhsT.ap, rhs=rhs.ap, start=start, stop=stop),
                     reads=_tts(lhsT, rhs), writes=[out.tt])

    def tr(self, out, in_, ident):
        self.fw.emit("pe", lambda e: e.transpose(out.ap, in_.ap, ident.ap),
                     reads=_tts(in_, ident), writes=[out.tt])

    def act(self, out, in_, func, bias=None, scale=None, accum=None):
        kw = {}
        if bias is not None:
            kw["bias"] = _a(bias)
        if scale is not None:
            kw["scale"] = _a(scale)
        if accum is not None:
            kw["accum_out"] = accum.ap
        self.fw.emit("act", lambda e: e.activation(out=out.ap, in_=in_.ap, func=func, **kw),
                     reads=_tts(in_, bias, scale), writes=_tts(out, accum))

    def tt(self, eng, out, in0, in1, op):
        self.fw.emit(eng, lambda e: e.tensor_tensor(out=out.ap, in0=in0.ap, in1=in1.ap, op=op),
                     reads=_tts(in0, in1), writes=[out.tt])

    def ts(self, eng, out, in0, s1, s2, op0, op1=None):
        if op1 is None:
            f = lambda e: e.tensor_scalar(out=out.ap, in0=in0.ap, scalar1=_a(s1), scalar2=None, op0=op0)
        else:
            f = lambda e: e.tensor_scalar(out=out.ap, in0=in0.ap, scalar1=_a(s1), scalar2=_a(s2),
                                          op0=op0, op1=op1)
        self.fw.emit(eng, f, reads=_tts(in0, s1, s2), writes=[out.tt])

    def stt(self, eng, out, in0, scalar, in1, op0, op1):
        self.fw.emit(eng, lambda e: e.scalar_tensor_tensor(out=out.ap, in0=in0.ap, scalar=_a(scalar),
                                                           in1=in1.ap, op0=op0, op1=op1),
                     reads=_tts(in0, scalar, in1), writes=[out.tt])

    def cp(self, eng, out, in_):
        self.fw.emit(eng, lambda e: e.tensor_copy(out=out.ap, in_=in_.ap), reads=[in_.tt], writes=[out.tt])

    def memset(self, eng, out, val):
        self.fw.emit(eng, lambda e: e.memset(out.ap, val), writes=[out.tt])

    def recip(self, out, in_):
        self.fw.emit("dve", lambda e: e.reciprocal(out=out.ap, in_=in_.ap), reads=[in_.tt], writes=[out.tt])

    def dma(self, queue, out, in_, xr=(), xw=()):
        xr, xw = list(xr), list(xw)
        if isinstance(out, View) and isinstance(in_, View):
            raise NotImplementedError
        if isinstance(out, View):
            self.fw.dma(queue, lambda e: e.dma_start(out=out.ap, in_=in_), out.tt, reads=xr, writes=[out.tt] + xw)
        elif isinstance(in_, View):
            self.fw.dma(queue, lambda e: e.dma_start(out=out, in_=in_.ap), in_.tt, reads=[in_.tt] + xr, writes=xw)
        else:
            self.fw.dma(queue, lambda e: e.dma_start(out=out, in_=in_), self.dummy, reads=xr,
                        writes=[self.dummy] + xw)

    def rsqrt(self, out, in_, scale, CS):
        self.act(out, in_, AF.Sqrt, bias=self.epsc[0:CS, 0:1], scale=scale)
        self.recip(out, out)


class G:
    pass


def bf(v):
    return v.m(lambda a: a.bitcast(BF16))


def build_program():
    B = Builder()
    fw = B.fw
    DD = B.D
    for n, s in [("xp", [2048, D]), ("xs", [16, D]), ("cT", [128, 8 * 17]), ("sret", [16, 4, 256, 512]),
                 ("sconv", [16, 30, D]), ("w_in", [D, 10240]), ("w_ada", [D, 6144]), ("b_ada", [1, 6144]),
                 ("w_conv_out", [D, D]), ("w_ret_out", [2048, D]), ("w_out", [D, D]),
                 ("w_gate", [D, 2816]), ("w_up", [D, 2816]), ("w_down", [2816, D]),
                 ("pp", [128, NPP]), ("gpost", [2, D]), ("tabk", [128, 2, 2048]),
                 ("tabq", [4, 128, 2, 2048]), ("tabs", [128, 10, 16]), ("dm", [128, 512]),
                 ("kdec", [128, 4]), ("ident", [128, 128]), ("sel", [17, 128]), ("diag16", [128, 256])]:
        B.din(n, s)
    for n, s in [("yp", [2048, D]), ("ys", [16, D]), ("rsp", [4, 256, 512]), ("rss", [16, 4, 256, 512]),
                 ("csp", [30, D]), ("css", [16, 30, D])]:
        B.dout(n, s)

    NBP = 46
    wscr = B.nc.dram_tensor("wscr", [NBP, 128, 4096], BF16, kind="Internal").ap()
    gscr = B.nc.dram_tensor("gscr", [2, 16, D], F32, kind="Internal").ap()
    wscr_tt = [TT(f"wscr{b}", "dram", b, b + 1, None) for b in range(NBP)]
    gscr_tt = [TT(f"gscr{b}", "dram", 1000 + b, 1001 + b, None) for b in range(2)]
    off = 0
    slots = []
    for i in range(NS):
        slots.append(fw.sb(f"slot{i}", [128, 8, 512], BF16, off))
        off += 8192
    S32 = []
    for i in range(8):
        S32.append(fw.sb(f"S32_{i}", [128, 512], F32, off))
        off += 2048
    S16 = []
    for i in range(8):
        S16.append(fw.sb(f"S16_{i}", [128, 512], BF16, off))
        off += 1024
    coff = [off]

    def calloc(name, shape, dt):
        n = int(np.prod(shape[1:])) * _DT_SIZE[dt]
        t = fw.sb(name, shape, dt, coff[0])
        coff[0] += (n + 31) // 32 * 32
        return t

    identb = calloc("identb", [128, 128], BF16)
    identf = calloc("identf", [128, 128], F32)
    onesb = calloc("onesb", [128, 128], BF16)
    dm = calloc("dm", [128, 512], F32)
    G1bc = calloc("G1bc", [128, D], F32)
    G2bc = calloc("G2bc", [128, D], F32)
    pp = calloc("pp", [128, NPP], F32)
    modT = calloc("modT", [128, 48, 17], F32)
    A1T = calloc("A1T", [128, 8, 17], F32)
    A2T = calloc("A2T", [128, 8, 17], F32)
    kdec = calloc("kdec", [128, 4], F32)
    diag16 = calloc("diag16", [128, 256], F32)
    sel = calloc("sel", [17, 128], F32)
    B.epsc = calloc("epsc", [128, 1], F32)
    B.dummy = calloc("dummy", [128, 1], F32)
    scT = calloc("scT", [128, 8 * 17], BF16)
    cvs_p = calloc("cvs_p", [128, 8, 16], F32)
    tabs = calloc("tabs", [128, 10, 16], F32)
    halo = calloc("halo", [128, 8, 32], BF16)
    assert coff[0] - off <= 20480, coff[0] - off
    off += 20480
    R0 = off
    RSZ = fw.ARENA - R0
    assert RSZ >= 102400, RSZ

    gpre1 = pp[:, 0:8]
    gpre2 = pp[:, 8:16]
    convb = pp[:, 16:24]
    lng = pp[:, 24:32]
    lnb = pp[:, 32:40]
    gng = pp[:, 40:56]
    cwT = pp[:, 56:56 + 248].m(lambda a: a.rearrange("p (c w) -> p c w", w=31))
    Bv1T = modT[:, 0:8, :]
    Bv2T = modT[:, 24:32, :]

    lg = np.log(1.0 - np.exp(np.linspace(np.log(1.0 / 32.0), np.log(1.0 / 512.0), NH)))
    gam = np.exp(lg)
    gam128 = np.exp(lg * 128.0)

    def mk_group(kind, tile):
        g = G()
        g.kind = kind
        g.tile = tile
        P = kind == "p"
        g.T = 512 if P else 16
        g.CS = 128 if P else 16
        g.NJ = 4 if P else 1
        T, CS, NJ = g.T, g.CS, g.NJ
        o = [R0]

        def al(name, shape, dt, at=None):
            if at is not None:
                o[0] = at
            n = int(np.prod(shape[1:])) * _DT_SIZE[dt]
            t = fw.sb(f"{kind}{name}", shape, dt, o[0])
            o[0] += (n + 31) // 32 * 32
            return t

        g.xs = None
        g.hT = [al(f"hT{k}", [128, T], BF16) for k in range(8)]
        g.cpart = [al(f"cp{k}", [128, T], BF16) for k in range(8)]
        PH = R0 + 16384 if P else o[0]
        g.sgc = [al(f"sgc{k}", [128, T], BF16, at=PH if k == 0 else None) for k in range(8)]
        g.siga = [al(f"siga{k}", [128, T], BF16) for k in range(4)]
        g.glu = [al(f"glu{k}", [128, 544 if P else 16], BF16 if P else F32) for k in range(8)]
        g.diag = [al(f"diag{k}", [128, 31, 128], BF16) for k in range(2)] if P else None
        g.cvT = [al(f"cvT{k}", [128, T], BF16) for k in range(8)]
        g.sq = [al(f"sq{k}", [128, T], BF16) for k in range(2)]
        g.mean = al("mean", [128, T], F32)
        g.rstd = al("rstd", [128, T], F32)
        g.msq = al("msq", [128, T], F32)
        g.lntmp = al("lntmp", [128, T], F32)
        g.lntmpb = al("lntmpb", [128, T], F32)
        g.lnT = [al(f"lnT{k}", [128, T], BF16) for k in range(8)]
        e_ln = o[0]
        g.gtail = al("gtail", [128, 8, 32], F32)
        g.ctail = al("ctail", [32, D], F32)
        if not P:
            g.strow = [al(f"strow{k}", [120, D], F32) for k in range(1)]
            g.cvs = al("cvs", [128, 8, 16], F32)
            g.cvtmp = al("cvtmp", [128, 120], F32)
        e1 = o[0]
        g.rT = [al(f"rT{k}", [128, 4, T], BF16, at=(PH if P else e_ln) if k == 0 else None) for k in range(4)]
        g.qk = [al(f"qk{k}", [128, T], BF16, at=(PH + 16384 if (P and k == 0) else None)) for k in range(16)]
        g.tmp = [al(f"tmp{k}", [128, T], F32, at=(PH + 68608 if (P and k == 0) else None)) for k in range(4)]
        g.tabk = al("tabk", [128, 2, T], F32)
        g.tabq = al("tabq", [128, 2, T], F32)
        HG = 2 if P else 1
        g.vv = [[al(f"v{hh}_{k}", [CS, 512], BF16, at=(PH + 36864 if (P and hh == 0 and k == 0) else None))
                 for k in range(NJ)] for hh in range(HG)]
        g.sgg = [[al(f"sg{hh}_{k}", [CS, 512], BF16) for k in range(NJ)] for hh in range(HG)]
        g.khh = [al(f"khat{hh}", [CS, NJ, 256], BF16) for hh in range(HG)]
        g.Pm = [al(f"Pm{k}", [128, 128], BF16) for k in range(2)]
        g.rn = [al(f"rn{k}", [CS, 512], BF16) for k in range(2)]
        g.r = [al(f"r{k}", [CS, 512], BF16) for k in range(6 if P else 2)]
        g.st6s = [al(f"st6_{k}", [128, 6], F32) for k in range(4)]
        g.mvs = [al(f"mv_{k}", [128, 2], F32) for k in range(4)]
        g.gsds = [al(f"gsd_{k}", [128, 1], F32) for k in range(4)]
        g.gnbs = [al(f"gnb_{k}", [128, 1], F32) for k in range(4)]
        g.st6 = al("st6", [128, 6], F32)
        g.mv = al("mv", [128, 2], F32)
        g.gsd = al("gsd", [128, 1], F32)
        g.gnb = al("gnb", [128, 1], F32)
        if not P:
            g.st = None
            g.km = [al(f"km{k}", [16, 256], BF16) for k in range(2)]
            g.qTm = al("qTm", [128, 2, 16, 16], BF16)
        e2 = max(o[0], PH + 84992) if P else o[0]
        g.sgr = [al(f"sgr{k}", [128, T], BF16, at=(PH + 16384 if P else e2) if k == 0 else None)
                 for k in range(8)]
        g.merged = [al(f"mg{k}", [128, T], BF16) for k in range(8)]
        g.mix = [al(f"mix{k}", [CS, D], F32) for k in range(NJ)]
        g.mtmp = al("mtmp", [128, T], F32)
        g.junk = al("junk", [CS, 512], BF16)
        g.ssm = al("ssm", [128, 8], F32)
        g.ss2 = al("ss2", [128, 4], F32)
        e3 = o[0]
        g.xn = [al(f"xn{k}", [CS, D], BF16, at=PH if k == 0 else None) for k in range(NJ)]
        g.aT = [al(f"aT{k}", [128, T], BF16) for k in range(22)]
        g.sgt = [al(f"sgt{k}", [128, T], BF16) for k in range(4)]
        g.ffo = [al(f"ffo{k}", [CS, D], F32) for k in range(NJ)]
        g.ssj = [al(f"ssj{k}", [128, 1], F32) for k in range(NJ)]
        g.rsj = [al(f"rsj{k}", [128, 1], F32) for k in range(NJ)]
        g.ntmp = al("ntmp", [128, T], F32)
        e4 = o[0]
        if P:
            g.xs1 = [al(f"xs1_{k}", [128, D], F32, at=PH + 53248 if k == 0 else None) for k in range(4)]
            g.xs0 = [al(f"xs0_{k}", [128, D], F32) for k in range(4)]
            e4 = max(e4, o[0])
        if not P:
            o[0] = max(e1, e2, e3, e4)
            g.xs0 = g.xs1 = [al("xss", [16, D], F32)]
            g.G1s = al("G1s", [16, D], F32)
            g.G2s = al("G2s", [16, D], F32)
            e4 = o[0]
        assert max(e1, e2, e3, e4) <= fw.ARENA, (e1, e2, e3, e4)
        g.end = max(e1, e2, e3, e4)
        return g

    gs = mk_group("s", 0)
    gp = mk_group("p", 0)

    so = gs.end
    modrows = fw.sb("modrows", [17, 6144], F32, so)
    bada = fw.sb("bada", [17, 6144], F32, so + 24576)
    gpostbc = fw.sb("gpostbc", [128, 2, D], F32, so + 49152)
    assert so + 49152 + 8192 + 1024 <= fw.ARENA
    cTf = fw.sb("cTf", [128, 8 * 17], F32, so + 57344)
    NST = 10
    gs.st = [fw.sb(f"st{k}", [128, 2, 512], F32, so + k * 4096) for k in range(NST)]
    gs.stb = [fw.sb(f"stb{k}", [128, 2, 512], BF16, so + NST * 4096 + k * 2048) for k in range(4)]
    st_issued = [0]

    def st_load_upto(n):
        while st_issued[0] < min(n, 64):
            i = st_issued[0]
            B.dma("sp", gs.st[i % NST][:, :, :],
                  DD["sret"][i % 16, i // 16].rearrange("(c p) v -> p c v", p=128))
            st_issued[0] += 1

    blocks = []

    def wblk(name, r0, nk, c0, ncols):
        blocks.append((DD[name][r0:r0 + nk * 128, c0:c0 + ncols], nk, ncols))
        return len(blocks) - 1

    issued = [0]

    blk_pb = {}

    def issue_upto(n):
        while issued[0] < min(n, len(blocks)):
            b = issued[0]
            src, nk, ncols = blocks[b]
            slot = slots[b % NS]
            pinfo = blk_pb.get(b)
            if pinfo is None or pinfo[0] == 0:
                B.dma("pool", slot[:, 0:nk, 0:ncols], src.rearrange("(k p) n -> p k n", p=128))
            else:
                B.dma("pool", slot[:, :, :].m(lambda a: a.rearrange("p k n -> p (k n)")), wscr[pinfo[1]],
                      xr=[wscr_tt[pinfo[1]]])
            issued[0] += 1

    def cache_block(b):
        pinfo = blk_pb.get(b)
        slot = slots[b % NS]
        if pinfo is not None and pinfo[0] == 0:
            B.dma("sp", wscr[pinfo[1]], slot[:, :, :].m(lambda a: a.rearrange("p k n -> p (k n)")),
                  xw=[wscr_tt[pinfo[1]]])

    def use_block(b, cache=True, keep=0):
        issue_upto(b + NS - keep)
        slot = slots[b % NS]
        pinfo = blk_pb.get(b)
        if cache and pinfo is not None and pinfo[0] == 0:
            B.dma("sp", wscr[pinfo[1]], slot[:, :, :].m(lambda a: a.rearrange("p k n -> p (k n)")),
                  xw=[wscr_tt[pinfo[1]]])
        return slot

    sched = {"ada": [wblk("w_ada", 0, 8, c * 512, 512) for c in range(12)]}
    passes = [("p", t) for t in range(4)] + [("s", 0)]
    for pi, _ in enumerate(passes):
        s = {}
        nb0 = len(blocks)
        s["gc"] = [wblk("w_in", 0, 8, 9216 + b * 512, 512) for b in range(2)]
        s["au"] = []
        for b in range(2):
            s["au"].append(wblk("w_in", 0, 8, 7168 + b * 512, 512))
            s["au"].append(wblk("w_in", 0, 8, 6144 + b * 512, 512))
        s["q"] = [wblk("w_in", 0, 8, b * 512, 512) for b in range(2)]
        s["k"] = [wblk("w_in", 0, 8, 1024 + b * 512, 512) for b in range(2)]
        s["wc"] = [wblk("w_conv_out", 0, 8, b * 512, 512) for b in range(2)]
        s["vg"] = []
        for h in range(4):
            s["vg"].append(wblk("w_in", 0, 8, 2048 + h * 512, 512))
            s["vg"].append(wblk("w_in", 0, 8, 4096 + h * 512, 512))
        s["gr"] = [wblk("w_in", 0, 8, 8192 + b * 512, 512) for b in range(2)]
        s["wr"] = [wblk("w_ret_out", rh * 1024, 8, ch * 512, 512) for ch in range(2) for rh in range(2)]
        s["wo"] = [wblk("w_out", 0, 8, b * 512, 512) for b in range(2)]
        s["gu"] = []
        for i in range(6):
            nc_ = 512 if i < 5 else 256
            s["gu"].append(wblk("w_gate", 0, 8, i * 512, nc_))
            s["gu"].append(wblk("w_up", 0, 8, i * 512, nc_))
        s["wd"] = [wblk("w_down", rb * 1024, 8 if rb < 2 else 6, ch * 512, 512)
                   for ch in range(2) for rb in range(3)]
        sched[pi] = s
        assert len(blocks) - nb0 == NBP
        for i in range(NBP):
            blk_pb[nb0 + i] = (pi, i)

    sp = "sp"
    B.dma(sp, pp[:, :], DD["pp"])
    B.dma(sp, identf[:, :], DD["ident"])
    B.dma("pool", identb[:, :], DD["ident"])
    B.dma(sp, dm[:, :], DD["dm"])
    B.dma(sp, kdec[:, :], DD["kdec"])
    B.dma(sp, diag16[:, :], DD["diag16"])
    B.dma(sp, sel[:, :], DD["sel"])
    B.dma(sp, cTf[:, :], DD["cT"])
    B.dma(sp, tabs[:, :, :], DD["tabs"])
    B.dma(sp, bada[:, :], DD["b_ada"].partition_broadcast(17))
    for i in range(2):
        B.dma(sp, gpostbc[:, i, :], DD["gpost"][i:i + 1, :].partition_broadcast(128))
    B.memset("pool", onesb[:, :], 1.0)
    B.memset("pool", halo[:, :, :], 0.0)
    B.memset("pool", B.epsc[:, :], EPS)
    for i in range(8):
        B.memset("pool", S32[i][:, :], 0.0)
        B.memset("pool", S16[i][:, :], 0.0)
    issue_upto(NS - 1)
    B.dma(sp, DD["css"][:, 0:29, :], DD["sconv"][:, 1:30, :])

    for rt in range(4):
        strow = gs.strow[0]
        B.dma(sp, strow[:, :], DD["sconv"][rt * 4:(rt + 1) * 4, :, :].rearrange("s w d -> (s w) d"))
        for c in range(8):
            bank = fw.ps()
            B.tr(bank[:, 0:120], strow[:, c * 128:(c + 1) * 128], identf[0:120, 0:120])
            B.tt("dve", gs.cvtmp[:, :].m(lambda a: a.rearrange("p (s w) -> p s w", w=30)),
                 bank[:, 0:120].m(lambda a: a.rearrange("p (s w) -> p s w", w=30)),
                 cwT.m(lambda a: a[:, c, 0:30].unsqueeze(1).broadcast_to([128, 4, 30])), ALU.mult)
            B.fw.emit("dve", lambda e, c=c, rt=rt: e.reduce_sum(
                out=cvs_p[:, c, rt * 4:(rt + 1) * 4].ap,
                in_=gs.cvtmp[:, :].ap.rearrange("p (s w) -> p s w", w=30),
                axis=mybir.AxisListType.X), reads=[gs.cvtmp], writes=[cvs_p])
    B.act(scT[:, :], cTf[:, :], AF.Silu)
    for c, b in enumerate(sched["ada"]):
        slot = use_block(b)
        bank = fw.ps()
        for k in range(8):
            B.mm(bank[0:17, :], scT[:, k * 17:(k + 1) * 17], slot[:, k, :], k == 0, k == 7)
        B.tt("dve", modrows[:, c * 512:(c + 1) * 512], bank[0:17, :], bada[:, c * 512:(c + 1) * 512], ALU.add)
    for half in range(2):
        bank = fw.ps()
        for q in range(24):
            cq = half * 24 + q
            B.tr(bank[:, q * 17:(q + 1) * 17], modrows[0:17, cq * 128:(cq + 1) * 128], identf[0:17, 0:17])
        B.cp("dve", modT[:, half * 24:(half + 1) * 24, :],
             bank[:, 0:408].m(lambda a: a.rearrange("p (q s) -> p q s", s=17)))
    for (AT, c0, gp_) in ((A1T, 8, gpre1), (A2T, 32, gpre2)):
        B.ts("dve", AT[:, :, :], modT[:, c0:c0 + 8, :], 1.0, None, ALU.add)
        B.tt("dve", AT[:, :, :], AT[:, :, :],
             gp_.m(lambda a: a.unsqueeze(2).broadcast_to([128, 8, 17])), ALU.mult)
    for (Gbc, Gs, c0, gi) in ((G1bc, gs.G1s, 2048, 0), (G2bc, gs.G2s, 5120, 1)):
        for ch in range(2):
            bank = fw.ps()
            B.mm(bank[:, :], sel[:, :], modrows[:, c0 + ch * 512:c0 + (ch + 1) * 512], True, True)
            B.tt("dve", Gbc[:, ch * 512:(ch + 1) * 512], bank[:, :], gpostbc[:, gi, ch * 512:(ch + 1) * 512],
                 ALU.mult)
        B.tt("dve", Gs[:, :], modrows[0:16, c0:c0 + 1024], gpostbc[0:16, gi, :], ALU.mult)
        B.dma(sp, gscr[gi], Gs[:, :], xw=[gscr_tt[gi]])

    def norm_to_hT(g, AT, BT, xs):
        T, CS, NJ = g.T, g.CS, g.NJ
        tbk = [fw.ps(hold=True) for _ in range(4)]
        tbb = [bf(b_[:, :]) for b_ in tbk]
        for j in range(NJ):
            B.act(g.xn[j][:, :], xs[j][0:CS, :], AF.Square, accum=g.ssj[j][0:CS, :])
            B.rsqrt(g.rsj[j][0:CS, :], g.ssj[j][0:CS, :], 1.0 / D, CS)
            B.ts("dve", g.xn[j][:, :], xs[j][0:CS, :], g.rsj[j][0:CS, 0:1], None, ALU.mult)
            for k in range(8):
                c0 = (k % 2) * 512 + j * CS
                B.tr(tbb[k // 2].m(lambda a: a[:, c0:c0 + CS]), g.xn[j][:, k * 128:(k + 1) * 128],
                     identb[0:CS, 0:CS])
        for k in range(8):
            c0 = (k % 2) * 512
            src = tbb[k // 2].m(lambda a: a[:, c0:c0 + T])
            if g.kind == "p":
                B.act(g.hT[k][:, :], src, AF.Identity, scale=AT[:, k, 16:17], bias=BT[:, k, 16:17])
            else:
                B.tt("dve", g.ntmp[:, :], src, AT[:, k, 0:16], ALU.mult)
                B.tt("dve", g.hT[k][:, :], g.ntmp[:, :], BT[:, k, 0:16], ALU.add)
        for b_ in tbk:
            fw.release(b_)

    def proj_fm_kouter(g, slot, nk, rhs, nm, evac):
        banks = [fw.ps(hold=True) for _ in range(nm)]
        for k in range(nk):
            for m in range(nm):
                B.mm(banks[m][:, 0:g.T], slot[:, k, m * 128:(m + 1) * 128], rhs(k), k == 0, k == nk - 1)
        for m in range(nm):
            evac(m, banks[m][:, 0:g.T])
            fw.release(banks[m])

    def proj_fm(g, slot, nk, rhs, nm, evac, banks=None, first=True, last=True):
        for m in range(nm):
            bank = banks[m] if banks is not None else fw.ps()
            for k in range(nk):
                B.mm(bank[:, 0:g.T], slot[:, k, m * 128:(m + 1) * 128], rhs(k),
                     first and k == 0, last and k == nk - 1)
            if last and evac is not None:
                evac(m, bank[:, 0:g.T])

    def proj_tm(g, slot, nk, lhs, ncols, evac, banks=None, first=True, last=True):
        CS = g.CS
        for j in range(g.NJ):
            bank = banks[j] if banks is not None else fw.ps()
            for k in range(nk):
                B.mm(bank[0:CS, 0:ncols], lhs(k, j), slot[:, k, 0:ncols],
                     first and k == 0, last and k == nk - 1)
            if last and evac is not None:
                evac(j, bank[0:CS, 0:ncols])

    def resid(g, src, ssx, Gp, Gs_, after=None):
        CS, NJ = g.CS, g.NJ
        Gv = Gp[:, :] if g.kind == "p" else Gs_[:, :]
        for j in range(NJ):
            B.act(g.xn[j][:, :], src[j][:, :], AF.Square, accum=g.ssj[j][0:CS, :])
            B.rsqrt(g.rsj[j][0:CS, :], g.ssj[j][0:CS, :], 1.0 / D, CS)
            B.stt("dve", src[j][:, :], src[j][:, :], g.rsj[j][0:CS, 0:1], Gv, ALU.mult, ALU.mult)
            B.tt("dve", g.xs1[j][0:CS, :], g.xs1[j][0:CS, :], src[j][:, :], ALU.add)
            if after is not None:
                after(j)

    def rotary(g, bA, bB, cos, sin, outA, outB):
        t = g.tmp
        B.tt("dve", t[0][:, :], bA, cos, ALU.mult)
        B.tt("dve", t[1][:, :], bB, sin, ALU.mult)
        B.tt("pool", outA[:, :], t[0][:, :], t[1][:, :], ALU.subtract)
        B.tt("dve", t[2][:, :], bB, cos, ALU.mult)
        B.tt("dve", t[3][:, :], bA, sin, ALU.mult)
        B.tt("pool", outB[:, :], t[2][:, :], t[3][:, :], ALU.add)

    def gn_gate(g, ybank, sgv, rn, r):
        CS = g.CS
        B.fw.emit("dve", lambda e: e.bn_stats(out=g.st6[0:CS, :].ap, in_=ybank.ap),
                  reads=[ybank.tt], writes=[g.st6])
        B.fw.emit("dve", lambda e: e.bn_aggr(out=g.mv[0:CS, :].ap, in_=g.st6[0:CS, :].ap),
                  reads=[g.st6], writes=[g.mv])
        B.rsqrt(g.gsd[0:CS, :], g.mv[0:CS, 1:2], 1.0, CS)
        B.stt("dve", g.gnb[0:CS, :], g.mv[0:CS, 0:1], -1.0, g.gsd[0:CS, :], ALU.mult, ALU.mult)
        B.act(rn[:, :], ybank, AF.Identity, scale=g.gsd[0:CS, 0:1], bias=g.gnb[0:CS, 0:1])
        B.tt("pool", r[:, :], rn[:, :], sgv, ALU.mult)

    def sq_evac(g, dst, bankv, acc):
        B.act(dst, bankv, AF.Copy)

    class _Stop(Exception):
        pass

    def front(g, t):
        if g.kind == "p":
            for j in range(4):
                B.dma(sp, g.xs0[j][:, :], DD["xp"][t * 512 + j * 128:t * 512 + (j + 1) * 128, :])
        else:
            B.dma(sp, g.xs0[0][:, :], DD["xs"])
            B.dma(sp, g.G1s[:, :], gscr[0], xr=[gscr_tt[0]])
            B.dma(sp, g.G2s[:, :], gscr[1], xr=[gscr_tt[1]])
            st_load_upto(NST - 1)
        norm_to_hT(g, A1T, Bv1T, g.xs0)

    def run_pass(pi, g):
        try:
            run_pass_(pi, g)
        except _Stop:
            pass

    def run_pass_(pi, g):
        import os
        kph = float(os.environ.get("KPH", "99")) if pi == int(os.environ.get("KSTOP", "99")) - 1 else 99
        P = g.kind == "p"
        T, CS, NJ = g.T, g.CS, g.NJ
        t = g.tile
        s = sched[pi]
        first_tile = P and t == 0
        last_tile = P and t == 3
        hTk = lambda k: g.hT[k][:, :]
        hTkj = lambda k, j: g.hT[k][:, j * CS:(j + 1) * CS]

        if (not P) or t == 0:
            front(g, t)

        if kph <= 0:
            raise _Stop()
        for b in range(2):
            slot = use_block(s["gc"][b])
            proj_fm(g, slot, 8, hTk, 4, lambda m, bv: B.act(g.sgc[b * 4 + m][:, :], bv, AF.Sigmoid))
        if P:
            for c in range(8):
                B.cp("pool", g.glu[c][:, 0:30], halo[:, c, 0:30])
        for b in range(2):
            slot = use_block(s["au"][2 * b])
            proj_fm(g, slot, 8, hTk, 4, lambda m, bv: B.act(g.siga[m][:, :], bv, AF.Sigmoid))
            slot = use_block(s["au"][2 * b + 1])

            def ev_u(m, bv):
                c = b * 4 + m
                if P:
                    B.tt("dve", g.glu[c][:, 30:30 + T], bv, g.siga[m][:, :], ALU.mult)
                    if last_tile:
                        B.tt("dve", g.gtail[:, c, 0:30], bv.m(lambda a: a[:, 482:512]),
                             g.siga[m][:, 482:512], ALU.mult)
                else:
                    B.tt("dve", g.glu[c][:, :], bv, g.siga[m][:, :], ALU.mult)
            proj_fm(g, slot, 8, hTk, 4, ev_u)

        bsum = fw.ps(hold=True)
        bsq = fw.ps(hold=True)
        if P:
            for c in range(8):
                dg = g.diag[c % 2]
                B.tt("pool", dg[:, :, :], identb[:, :].m(lambda a: a.unsqueeze(1).broadcast_to([128, 31, 128])),
                     cwT.m(lambda a: a[:, c, :].unsqueeze(2).broadcast_to([128, 31, 128])), ALU.mult)
                bank = fw.ps()
                for w in range(31):
                    B.mm(bank[:, :], dg[:, w, :], g.glu[c][:, w:w + 512], w == 0, w == 30)
                B.act(g.cvT[c][:, :], bank[:, :], AF.Identity, bias=convb.m(lambda a: a[:, c:c + 1]))
                sqb = g.sq[c % 2]
                B.act(sqb[:, :], bank[:, :], AF.Square, bias=convb.m(lambda a: a[:, c:c + 1]))
                B.mm(bsum[:, 0:T], onesb[:, :], g.cvT[c][:, :], c == 0, c == 7)
                B.mm(bsq[:, 0:T], onesb[:, :], sqb[:, :], c == 0, c == 7)
            if not last_tile:
                for c in range(8):
                    B.cp("pool", halo[:, c, 0:30], g.glu[c][:, 512:542])
            else:
                for half in range(2):
                    bank = fw.ps()
                    for q in range(4):
                        c = half * 4 + q
                        B.tr(bank[0:30, q * 128:(q + 1) * 128], g.gtail[:, c, 0:30], identf[:, :])
                    B.cp("dve", g.ctail[0:30, half * 512:(half + 1) * 512], bank[0:30, :])
                B.dma(sp, DD["csp"], g.ctail[0:30, :])
        else:
            for c in range(8):
                B.stt("dve", g.cvs[:, c, :], g.glu[c][:, :], cwT.m(lambda a: a[:, c, 30:31]), cvs_p[:, c, :],
                      ALU.mult, ALU.add)
                B.act(g.cvT[c][:, :], g.cvs[:, c, :], AF.Identity, bias=convb.m(lambda a: a[:, c:c + 1]))
                sqb = g.sq[c % 2]
                B.act(sqb[:, :], g.cvs[:, c, :], AF.Square, bias=convb.m(lambda a: a[:, c:c + 1]))
                B.mm(bsum[:, 0:T], onesb[:, :], g.cvT[c][:, :], c == 0, c == 7)
                B.mm(bsq[:, 0:T], onesb[:, :], sqb[:, :], c == 0, c == 7)
            for half in range(2):
                bank = fw.ps()
                for q in range(4):
                    c = half * 4 + q
                    B.tr(bank[0:16, q * 128:(q + 1) * 128], g.glu[c][:, :], identf[:, :])
                B.cp("dve", g.ctail[0:16, half * 512:(half + 1) * 512], bank[0:16, :])
            B.dma(sp, DD["css"][:, 29, :], g.ctail[0:16, :])
        B.ts("dve", g.mean[:, :], bsum[:, 0:T], 1.0 / D, None, ALU.mult)
        B.tt("dve", g.msq[:, :], g.mean[:, :], g.mean[:, :], ALU.mult)
        B.stt("dve", g.rstd[:, :], bsq[:, 0:T], 1.0 / D, g.msq[:, :], ALU.mult, ALU.subtract)
        fw.release(bsum)
        fw.release(bsq)
        B.rsqrt(g.rstd[:, :], g.rstd[:, :], 1.0, 128)
        def ln_chunk(c):
            lt = g.lntmp if c % 2 == 0 else g.lntmpb
            ltb = fw.ps()
            B.tt("dve", ltb[:, 0:T], g.cvT[c][:, :], g.mean[:, :], ALU.subtract)
            B.tt("dve", lt[:, :], ltb[:, 0:T], g.rstd[:, :], ALU.mult)
            B.act(g.lnT[c][:, :], lt[:, :], AF.Silu, scale=lng.m(lambda a: a[:, c:c + 1]),
                  bias=lnb.m(lambda a: a[:, c:c + 1]))
        ln_next = [0]

        def ln_some(n):
            for _ in range(n):
                if ln_next[0] < 8:
                    ln_chunk(ln_next[0])
                    ln_next[0] += 1

        if kph <= 1:
            raise _Stop()
        tbuf = [g.tabq, g.tabk]

        def tab_load(i):
            if not P:
                return
            if i < 4:
                B.dma(sp, tbuf[i % 2][:, :, :], DD["tabq"][i, :, :, t * 512:(t + 1) * 512])
            else:
                B.dma(sp, tbuf[0][:, :, :], DD["tabk"][:, :, t * 512:(t + 1) * 512])
        if P:
            tab_load(0)
            tab_load(1)
            kcos, ksin = tbuf[0][:, 0, :], tbuf[0][:, 1, :]
        else:
            kcos, ksin = tabs[:, 0, :], tabs[:, 1, :]
        for b in range(2):
            slot = use_block(s["q"][b])
            for pr in range(2):
                h = b * 2 + pr
                if P:
                    if h >= 1:
                        tab_load(h + 1)
                    qcos, qsin = tbuf[h % 2][:, 0, :], tbuf[h % 2][:, 1, :]
                else:
                    qcos, qsin = tabs[:, 2 + 2 * h, :], tabs[:, 3 + 2 * h, :]
                ln_some(1)
                bks = [fw.ps(), fw.ps()]
                for mm_ in range(2):
                    m = pr * 2 + mm_
                    for k in range(8):
                        B.mm(bks[mm_][:, 0:T], slot[:, k, m * 128:(m + 1) * 128], hTk(k), k == 0, k == 7)
                rotary(g, bks[0][:, 0:T], bks[1][:, 0:T], qcos, qsin, g.qk[h * 2], g.qk[h * 2 + 1])
        for b in range(2):
            slot = use_block(s["k"][b])
            for pr in range(2):
                h = b * 2 + pr
                ln_some(1)
                bks = [fw.ps(), fw.ps()]
                for mm_ in range(2):
                    m = pr * 2 + mm_
                    for k in range(8):
                        B.mm(bks[mm_][:, 0:T], slot[:, k, m * 128:(m + 1) * 128], hTk(k), k == 0, k == 7)
                rotary(g, bks[0][:, 0:T], bks[1][:, 0:T], kcos, ksin, g.qk[8 + h * 2], g.qk[8 + h * 2 + 1])

        ln_some(8)
        for b in range(2):
            slot = use_block(s["wc"][b])
            (proj_fm_kouter if b == 0 else proj_fm)(
                g, slot, 8, lambda k: g.lnT[k][:, :], 4,
                lambda m, bv: B.tt("dve", g.cpart[b * 4 + m][:, :], bv, g.sgc[b * 4 + m][:, :], ALU.mult))
        pend = []
        LAG = 4
        rcount = [0]

        def flush(n):
            while len(pend) > n:
                pend.pop(0)()

        def head_proj(h, hh):
            kT = [g.qk[8 + h * 2], g.qk[8 + h * 2 + 1]]
            slot = use_block(s["vg"][2 * h])
            proj_tm(g, slot, 8, hTkj, 512, lambda j, bv: B.act(g.vv[hh][j][:, :], bv, AF.Copy))
            slot = use_block(s["vg"][2 * h + 1])
            proj_tm(g, slot, 8, hTkj, 512, lambda j, bv: B.act(g.sgg[hh][j][:, :], bv, AF.Silu))
            bank = fw.ps()
            bb = bf(bank[:, :])
            for j in range(NJ):
                for half in range(2):
                    o0 = j * 256 + half * 128
                    B.tr(bb.m(lambda a: a[0:CS, o0:o0 + 128]), kT[half][:, j * CS:(j + 1) * CS], identb[:, :])
            kflat = g.khh[hh][:, :, :].m(lambda a: a.rearrange("p j d -> p (j d)"))
            if P:
                B.ts("dve", kflat, bb.m(lambda a: a[:, 0:1024]), kdec[:, h:h + 1], None, ALU.mult)
            else:
                B.cp("dve", kflat, bb.m(lambda a: a[0:16, 0:256]))

        pendB = []
        gnr = []
        for k in range(4):
            gnr.append(dict(st6=g.st6s[k], mv=g.mvs[k], gsd=g.gsds[k], gnb=g.gnbs[k]))

        def flushB(n):
            while len(pendB) > n:
                pendB.pop(0)()

        ccount = [0]

        def chunk(h, hh, j):
            qT = [g.qk[h * 2], g.qk[h * 2 + 1]]
            kT = [g.qk[8 + h * 2], g.qk[8 + h * 2 + 1]]
            v, sg, khat = g.vv[hh], g.sgg[hh], g.khh[hh]
            cs = slice(j * 128, (j + 1) * 128)
            n = rcount[0]
            rcount[0] += 1
            bk = fw.banks
            sb_ = bk[0]
            yb = bk[1 + n % 3]
            for half in range(2):
                B.mm(sb_[:, 0:128], kT[half][:, cs], qT[half][:, cs], half == 0, half == 1)
            Pm = g.Pm[n % 2]
            B.tt("dve", Pm[:, :], sb_[:, 0:128], dm[:, h * 128:(h + 1) * 128], ALU.mult)
            for half in range(2):
                sbk = bk[4 + half]
                B.mm(sbk[:, :], khat[:, j, half * 128:(half + 1) * 128], v[j][:, :], True, True)
            B.mm(yb[:, :], Pm[:, :], v[j][:, :], True, False)
            for half in range(2):
                B.mm(yb[:, :], qT[half][:, cs], S16[h * 2 + half][:, :], False, half == 1)
            for half in range(2):
                sbk = bk[4 + half]
                B.stt("dve", S32[h * 2 + half][:, :], S32[h * 2 + half][:, :], float(gam128[h]),
                      sbk[:, :], ALU.mult, ALU.add)
                if half == 0:
                    B.act(S16[h * 2 + half][:, :], S32[h * 2 + half][:, :], AF.Copy)
                else:
                    B.cp("pool", S16[h * 2 + half][:, :], S32[h * 2 + half][:, :])
            rn = g.rn[n % 2]
            r = g.r[n % len(g.r)]
            gb = gnr[n % 4]
            ybv = yb[:, :]
            B.fw.emit("dve", lambda e: e.bn_stats(out=gb["st6"][:, :].ap, in_=ybv.ap),
                      reads=[ybv.tt], writes=[gb["st6"]])
            B.fw.emit("dve", lambda e: e.bn_aggr(out=gb["mv"][:, :].ap, in_=gb["st6"][:, :].ap),
                      reads=[gb["st6"]], writes=[gb["mv"]])
            B.act(gb["gsd"][:, :], gb["mv"][:, 1:2], AF.Sqrt, bias=B.epsc[:, 0:1], scale=1.0)

            def B2():
                B.recip(gb["gsd"][:, :], gb["gsd"][:, :])
                B.stt("dve", gb["gnb"][:, :], gb["mv"][:, 0:1], -1.0, gb["gsd"][:, :], ALU.mult, ALU.mult)
                B.act(rn[:, :], ybv, AF.Identity, scale=gb["gsd"][:, 0:1], bias=gb["gnb"][:, 0:1])
                B.tt("pool", r[:, :], rn[:, :], sg[j][:, :], ALU.mult)

                def C():
                    tb = bk[6 + ccount[0] % 2]
                    ccount[0] += 1
                    tbb = bf(tb[:, :])
                    for q4 in range(4):
                        B.tr(tbb.m(lambda a: a[:, q4 * 128:(q4 + 1) * 128]),
                             r[:, q4 * 128:(q4 + 1) * 128], identb[:, :])
                    B.cp("dve", g.rT[h][:, :, cs],
                         tbb.m(lambda a: a[:, 0:512].rearrange("p (q t) -> p q t", q=4)))
                pend.append(C)
            pendB.append(B2)
            flushB(1)
            flush(LAG)

        if P:
            for hp in range(0, NH, 2):
                flushB(0)
                for hh in range(2):
                    head_proj(hp + hh, hh)
                for j in range(NJ):
                    for hh in range(2):
                        chunk(hp + hh, hh, j)
                if last_tile:
                    for hh in range(2):
                        h = hp + hh
                        for half in range(2):
                            B.dma(sp, DD["rsp"][h, half * 128:(half + 1) * 128, :], S32[h * 2 + half][:, :])
            bkx = fw.banks
            grb = [[bkx[0], bkx[4], bkx[5], bkx[6]], [bkx[7], bkx[0], bkx[4], bkx[5]]]
            for b in range(2):
                slot = use_block(s["gr"][b])
                proj_fm(g, slot, 8, hTk, 4, lambda m, bv: B.act(g.sgr[b * 4 + m][:, :], bv, AF.Sigmoid),
                        banks=grb[b])
            flushB(0)
            flush(0)
        else:
            for h in range(NH):
                qT = [g.qk[h * 2], g.qk[h * 2 + 1]]
                head_proj(h, 0)
                for c in range(2):
                    B.tt("dve", g.qTm[:, c, :, :],
                         qT[c][:, :].m(lambda a: a.unsqueeze(1).broadcast_to([128, 16, 16])),
                         diag16[:, :].m(lambda a: a.rearrange("p (s t) -> p s t", t=16)), ALU.mult)
                ob = fw.ps(hold=True)
                pend_o = []

                def flush_o(n):
                    while len(pend_o) > n:
                        pend_o.pop(0)()
                for smp in range(16):
                    gi = h * 16 + smp
                    st_load_upto(gi + NST - 1)
                    st = g.st[gi % NST]
                    km = g.km[smp % 2]
                    B.ts("dve", km[:, :], g.khh[0][:, 0, :], identf[0:16, smp:smp + 1], None, ALU.mult)
                    for c in range(2):
                        kb = fw.ps()
                        B.mm(kb[:, :], km[:, c * 128:(c + 1) * 128], g.vv[0][0][:, :], True, True)
                        B.stt("dve", st[:, c, :], st[:, c, :], float(gam[h]), kb[:, :], ALU.mult, ALU.add)
                    stb = g.stb[gi % 4]
                    B.act(stb[:, :, :], st[:, :, :], AF.Copy)
                    B.dma(sp, DD["rss"][smp, h].rearrange("(c p) v -> p c v", p=128), st[:, :, :])

                    def mk_o(smp=smp, stb=stb):
                        def o_():
                            for c in range(2):
                                B.mm(ob[0:16, :], g.qTm[:, c, smp, :], stb[:, c, :], smp == 0 and c == 0,
                                     smp == 15 and c == 1)
                        return o_
                    pend_o.append(mk_o())
                    flush_o(2)
                flush_o(0)
                rn, r = g.rn[h % 2], g.r[h % 2]
                gn_gate(g, ob[0:16, :], g.sgg[0][0][:, :], rn, r)
                fw.release(ob)
                tb = fw.ps()
                tbb = bf(tb[:, :])
                for q4 in range(4):
                    B.tr(tbb.m(lambda a: a[:, q4 * 16:(q4 + 1) * 16]), r[:, q4 * 128:(q4 + 1) * 128],
                         identb[0:16, 0:16])
                B.act(g.rT[h][:, :, :], tbb.m(lambda a: a[:, 0:64].rearrange("p (q t) -> p q t", q=4)), AF.Copy)

        if kph <= 2:
            raise _Stop()
        if P:
            for j in range(4):
                B.dma(sp, g.xs1[j][:, :], DD["xp"][t * 512 + j * 128:t * 512 + (j + 1) * 128, :])
        if not P:
            for b in range(2):
                slot = use_block(s["gr"][b])
                proj_fm(g, slot, 8, hTk, 4, lambda m, bv: B.act(g.sgr[b * 4 + m][:, :], bv, AF.Sigmoid))
        if kph <= 2.2:
            raise _Stop()
        for ch in range(2):
            banks = [fw.ps(hold=True) for _ in range(4)]
            for rh in range(2):
                slot = use_block(s["wr"][ch * 2 + rh], cache=False)
                if pi == 0:
                    for k8 in range(8):
                        B.act(slot[:, k8, :], slot[:, k8, :], AF.Copy,
                              scale=gng.m(lambda a: a[:, rh * 8 + k8:rh * 8 + k8 + 1]))
                    cache_block(s["wr"][ch * 2 + rh])

                def ev_r(m, bv):
                    c = ch * 4 + m
                    B.tt("dve", g.mtmp[:, :], bv, g.sgr[c][:, :], ALU.mult)
                    B.tt("dve", g.merged[c][:, :], g.mtmp[:, :], g.cpart[c][:, :], ALU.add)
                proj_fm(g, slot, 8, lambda k: g.rT[(rh * 8 + k) // 4][:, (rh * 8 + k) % 4, :], 4, ev_r,
                        banks=banks, first=rh == 0, last=rh == 1)
            for bk in banks:
                fw.release(bk)
        if kph <= 2.5:
            raise _Stop()
        wslots = [use_block(s["wo"][0]), use_block(s["wo"][1], keep=1)]
        for j in range(NJ):
            for ch in range(2):
                bank = fw.ps()
                for k in range(8):
                    B.mm(bank[0:CS, 0:512], g.merged[k][:, j * CS:(j + 1) * CS], wslots[ch][:, k, 0:512],
                         k == 0, k == 7)
                sq_evac(g, g.mix[j][:, ch * 512:(ch + 1) * 512], bank[0:CS, 0:512], None)
        if kph <= 2.8:
            raise _Stop()
        resid(g, g.mix, g.ssm, G1bc, None if P else g.G1s)

        if kph <= 3:
            raise _Stop()
        norm_to_hT(g, A2T, Bv2T, g.xs1)
        for i in range(6):
            nm = 4 if i < 5 else 2
            slot = use_block(s["gu"][2 * i])
            (proj_fm_kouter if i == 0 else proj_fm)(g, slot, 8, hTk, nm,
                                                    lambda m, bv: B.act(g.sgt[m][:, :], bv, AF.Silu))
            slot = use_block(s["gu"][2 * i + 1])
            proj_fm(g, slot, 8, hTk, nm,
                    lambda m, bv: B.tt("dve", g.aT[i * 4 + m][:, :], bv, g.sgt[m][:, :], ALU.mult))
        if P and t < 3:
            front(g, t + 1)
        for ch in range(2):
            banks = [fw.ps(hold=True) for _ in range(NJ)]
            for rb in range(3):
                slot = use_block(s["wd"][ch * 3 + rb])
                nk = 8 if rb < 2 else 6
                proj_tm(g, slot, nk, lambda k, j: g.aT[rb * 8 + k][:, j * CS:(j + 1) * CS], 512,
                        lambda j, bv: sq_evac(g, g.ffo[j][:, ch * 512:(ch + 1) * 512], bv,
                                              g.ssm[0:CS, j * 2 + ch:j * 2 + ch + 1]),
                        banks=banks, first=rb == 0, last=rb == 2)
            for bk in banks:
                fw.release(bk)
        if P:
            resid(g, g.ffo, g.ssm, G2bc, None,
                  after=lambda j: B.dma(sp, DD["yp"][t * 512 + j * 128:t * 512 + (j + 1) * 128, :],
                                        g.xs1[j][:, :]))
        else:
            resid(g, g.ffo, g.ssm, G2bc, g.G2s, after=lambda j: B.dma(sp, DD["ys"], g.xs1[0][:, :]))

    import os
    kstop = int(os.environ.get("KSTOP", "99"))
    for pi, (kind, t) in enumerate(passes):
        if pi >= kstop:
            break
        if kind == "s":
            run_pass(pi, gs)
        else:
            gp.tile = t
            run_pass(pi, gp)

    fw.final_wait("sp")
    with ExitStack() as st:
        fw.generate(st)
    return B.nc


_CACHE = {}


def _consts():
    if "c" in _CACHE:
        return _CACHE["c"]
    lg = np.log(1.0 - np.exp(np.linspace(np.log(1.0 / 32.0), np.log(1.0 / 512.0), NH)))
    inv_freq = (np.float32(10000.0) ** (-np.arange(0, 256, 2, dtype=np.float32) / np.float32(256.0)))
    inv_freq = inv_freq.astype(np.float32)
    pos = np.arange(2048, dtype=np.float32)
    ang = (inv_freq[:, None] * pos[None, :]).astype(np.float32).astype(np.float64)
    cos, sin = np.cos(ang), np.sin(ang)
    tabk = np.stack([cos, sin], axis=1).astype(np.float32)
    i_in = (np.arange(2048) % 128).astype(np.float64)
    tabq = np.zeros((4, 128, 2, 2048), np.float32)
    for h in range(NH):
        dec = np.exp(lg[h] * (i_in + 1.0)) / 16.0
        tabq[h, :, 0, :] = cos * dec[None, :]
        tabq[h, :, 1, :] = sin * dec[None, :]
    angs = (inv_freq * np.float32(16384.0)).astype(np.float32).astype(np.float64)
    tabs = np.zeros((128, 10, 16), np.float32)
    tabs[:, 0, :] = np.cos(angs)[:, None]
    tabs[:, 1, :] = np.sin(angs)[:, None]
    for h in range(NH):
        tabs[:, 2 + 2 * h, :] = (np.cos(angs) / 16.0)[:, None]
        tabs[:, 3 + 2 * h, :] = (np.sin(angs) / 16.0)[:, None]
    j = np.arange(128, dtype=np.float64)
    dm = np.zeros((128, 4, 128), np.float32)
    kdec = np.zeros((128, 4), np.float32)
    mask = (j[None, :] >= j[:, None])
    for h in range(NH):
        dm[:, h, :] = np.exp(-lg[h] * (j + 1.0))[:, None] * mask
        kdec[:, h] = np.exp(lg[h] * (127.0 - j))
    sel = np.zeros((17, 128), np.float32)
    sel[16, :] = 1.0
    diag16 = np.broadcast_to(np.eye(16, dtype=np.float32).reshape(1, 256), (128, 256)).copy()
    c = dict(tabk=tabk, tabq=tabq, tabs=tabs, dm=dm.reshape(128, 512), kdec=kdec,
             ident=np.eye(128, dtype=np.float32), sel=sel, diag16=diag16)
    _CACHE["c"] = c
    return c


def _fm(v):
    return np.ascontiguousarray(v.reshape(-1, 128).T)


def kernel(x_prompt, x_sample, c_prompt, c_sample, state_ret, state_conv, w_in, w_ada, b_ada,
           g_pre1, g_post1, g_pre2, g_post2, conv_w, conv_b, conv_ln_g, conv_ln_b, w_conv_out,
           ret_gn_g, w_ret_out, w_out, w_ffn_gate, w_ffn_up, w_ffn_down):
    f = lambda a: np.ascontiguousarray(np.asarray(a, dtype=np.float32))
    x_prompt, x_sample, c_prompt, c_sample = f(x_prompt), f(x_sample), f(c_prompt), f(c_sample)
    state_ret, state_conv = f(state_ret), f(state_conv)
    cst = _consts()
    if "nc" not in _CACHE:
        _CACHE["nc"] = build_program()
    nc = _CACHE["nc"]
    cw = f(conv_w)[0]
    cwT = np.ascontiguousarray(cw.T.reshape(8, 128, 31).transpose(1, 0, 2)).reshape(128, 248)
    pp = np.concatenate([_fm(f(g_pre1)[0]), _fm(f(g_pre2)[0]), _fm(f(conv_b)[0]), _fm(f(conv_ln_g)[0]),
                         _fm(f(conv_ln_b)[0]), _fm(f(ret_gn_g)[0]), cwT], axis=1).astype(np.float32)
    gpost = np.stack([f(g_post1)[0], f(g_post2)[0]], axis=0)
    shared = dict(w_in=f(w_in)[0], w_ada=f(w_ada)[0], b_ada=f(b_ada), w_conv_out=f(w_conv_out)[0],
                  w_ret_out=f(w_ret_out)[0], w_out=f(w_out)[0], w_gate=f(w_ffn_gate)[0],
                  w_up=f(w_ffn_up)[0], w_down=f(w_ffn_down)[0], pp=np.ascontiguousarray(pp), gpost=gpost,
                  **cst)
    in_maps = []
    for i in range(8):
        call = np.concatenate([c_sample[16 * i:16 * i + 16], c_prompt[i:i + 1]], axis=0)
        cT = np.ascontiguousarray(call.T.reshape(8, 128, 17).transpose(1, 0, 2)).reshape(128, 136)
        m = dict(shared)
        m.update(xp=x_prompt[i], xs=np.ascontiguousarray(x_sample[16 * i:16 * i + 16, 0, :]), cT=cT,
                 sret=state_ret[0, 16 * i:16 * i + 16], sconv=state_conv[0, 16 * i:16 * i + 16])
        in_maps.append(m)
    import os
    ncores = int(os.environ.get("NCORES", "8"))
    res = run_bass_kernel_spmd(nc, in_maps[:ncores], core_ids=list(range(ncores)))
    R = list(res.results) + [res.results[0]] * (8 - ncores)
    y_prompt = np.stack([R[i]["yp"] for i in range(8)], axis=0)
    y_sample = np.concatenate([R[i]["ys"] for i in range(8)], axis=0)[:, None, :]
    rsp = np.stack([R[i]["rsp"] for i in range(8)], axis=0)[None]
    rss = np.concatenate([R[i]["rss"] for i in range(8)], axis=0)[None]
    csp = np.stack([R[i]["csp"] for i in range(8)], axis=0)[None]
    css = np.concatenate([R[i]["css"] for i in range(8)], axis=0)[None]
    return (y_prompt.astype(np.float32), y_sample.astype(np.float32), rsp.astype(np.float32),
            rss.astype(np.float32), csp.astype(np.float32), css.astype(np.float32))
```
